# Optimizing a Trainium2 kernel written in Bass

```python
import jax, jax.numpy as jnp
from jax import lax
import numpy as np

D_MODEL = 1024
BATCH = 8
SEQ = 2048
DEPTH = 4

GRID_W = 64
N_Q_HEADS = 8
N_KV_HEADS = 2
GROUP = N_Q_HEADS // N_KV_HEADS
HEAD_DIM = 64
ATTN_WIDTH = N_Q_HEADS * HEAD_DIM
KV_WIDTH = N_KV_HEADS * HEAD_DIM
CONV_WIDTH = D_MODEL // 2
CONV_KERNEL = 31
Q_BLOCK = 128
ROPE_THETA = 10000.0
ROPE_AXIS_DIM = HEAD_DIM // 2
D_FF = -(-8 * D_MODEL // (3 * 256)) * 256
IN_WIDTH = ATTN_WIDTH + 2 * KV_WIDTH + 2 * CONV_WIDTH + 2 * D_MODEL
EPS = 1e-6

kernel_name = "hybrid_conformer_gqa_axialrope_adaln_encoder"


def rms_norm(x, g):
    xf = x.astype(jnp.float32)
    y = xf * lax.rsqrt(jnp.mean(xf * xf, axis=-1, keepdims=True) + EPS)
    return (y * g.astype(jnp.float32)).astype(x.dtype)


def layer_norm(x, g, b):
    xf = x.astype(jnp.float32)
    mu = jnp.mean(xf, axis=-1, keepdims=True)
    var = jnp.mean(jnp.square(xf - mu), axis=-1, keepdims=True)
    y = (xf - mu) * lax.rsqrt(var + EPS)
    return (y * g.astype(jnp.float32) + b.astype(jnp.float32)).astype(x.dtype)


def axial_rope_tables(seq_len):
    rows = seq_len // GRID_W
    row_pos = jnp.broadcast_to(jnp.arange(rows)[:, None], (rows, GRID_W)).reshape(-1).astype(jnp.float32)
    col_pos = jnp.broadcast_to(jnp.arange(GRID_W)[None, :], (rows, GRID_W)).reshape(-1).astype(jnp.float32)
    inv_freq = ROPE_THETA ** (-jnp.arange(0, ROPE_AXIS_DIM, 2, dtype=jnp.float32) / ROPE_AXIS_DIM)
    ang_r = row_pos[:, None] * inv_freq[None, :]
    ang_c = col_pos[:, None] * inv_freq[None, :]
    return jnp.cos(ang_r), jnp.sin(ang_r), jnp.cos(ang_c), jnp.sin(ang_c)


def rotate_segment(x, cos, sin):
    x1, x2 = jnp.split(x, 2, axis=-1)
    cos = cos[None, :, None, :]
    sin = sin[None, :, None, :]
    return jnp.concatenate([x1 * cos - x2 * sin, x1 * sin + x2 * cos], axis=-1)


def apply_axial_rope(x, tabs):
    cos_r, sin_r, cos_c, sin_c = tabs
    xf = x.astype(jnp.float32)
    out = jnp.concatenate([
        rotate_segment(xf[..., :ROPE_AXIS_DIM], cos_r, sin_r),
        rotate_segment(xf[..., ROPE_AXIS_DIM:], cos_c, sin_c)], axis=-1)
    return out.astype(x.dtype)


def blocked_gqa(q, k, v):
    b, s = q.shape[0], q.shape[1]
    n_blk = s // Q_BLOCK
    q = q * jnp.asarray(HEAD_DIM ** -0.5, q.dtype)
    qb = q.reshape(b, n_blk, Q_BLOCK, N_KV_HEADS, GROUP, HEAD_DIM).transpose(1, 0, 2, 3, 4, 5)

    def one_block(q_blk):
        scores = jnp.einsum('bqhgd,bkhd->bhgqk', q_blk, k).astype(jnp.float32)
        probs = jax.nn.softmax(scores, axis=-1).astype(v.dtype)
        return jnp.einsum('bhgqk,bkhd->bqhgd', probs, v)

    o = lax.map(one_block, qb)
    return o.transpose(1, 0, 2, 3, 4, 5).reshape(b, s, ATTN_WIDTH)


def depthwise_conv(u, w, bias):
    pad = CONV_KERNEL // 2
    y = lax.conv_general_dilated(u, w[:, None, :], window_strides=(1,), padding=[(pad, pad)],
                                 dimension_numbers=('NWC', 'WIO', 'NWC'),
                                 feature_group_count=CONV_WIDTH)
    return y + bias


def setup_inputs(seed: int = 0) -> dict:
    key = jax.random.key(seed)
    ks = jax.random.split(key, 24)
    f32 = jnp.float32
    nrm = lambda k, shape, scale: (jax.random.normal(k, shape, f32) * scale)
    gain = lambda k, shape: 1.0 + 0.02 * jax.random.normal(k, shape, f32)
    L, D = DEPTH, D_MODEL
    return {
        "x": nrm(ks[0], (BATCH, SEQ, D), 1.0),
        "c": nrm(ks[1], (BATCH, D), 1.0),
        "w_ada": nrm(ks[2], (L, D, 6 * D), 0.5 * D ** -0.5),
        "b_ada": nrm(ks[3], (L, 6 * D), 0.02),
        "norm_mix_g": gain(ks[4], (L, D)),
        "w_in": nrm(ks[5], (L, D, IN_WIDTH), D ** -0.5),
        "q_norm_g": gain(ks[6], (L, HEAD_DIM)),
        "k_norm_g": gain(ks[7], (L, HEAD_DIM)),
        "w_attn_o": nrm(ks[8], (L, ATTN_WIDTH, D), ATTN_WIDTH ** -0.5),
        "conv_dw": nrm(ks[9], (L, CONV_KERNEL, CONV_WIDTH), CONV_KERNEL ** -0.5),
        "conv_dw_b": nrm(ks[10], (L, CONV_WIDTH), 0.02),
        "conv_ln_g": gain(ks[11], (L, CONV_WIDTH)),
        "conv_ln_b": nrm(ks[12], (L, CONV_WIDTH), 0.02),
        "w_conv_o": nrm(ks[13], (L, CONV_WIDTH, D), CONV_WIDTH ** -0.5),
        "b_conv_o": nrm(ks[14], (L, D), 0.02),
        "w_out": nrm(ks[15], (L, D, D), D ** -0.5),
        "norm_ffn_g": gain(ks[16], (L, D)),
        "w_ffn_in": nrm(ks[17], (L, D, 2 * D_FF), D ** -0.5),
        "w_ffn_out": nrm(ks[18], (L, D_FF, D), D_FF ** -0.5),
        "final_norm_g": gain(ks[19], (D,)),
    }


def reference(x, c, w_ada, b_ada, norm_mix_g, w_in, q_norm_g, k_norm_g, w_attn_o,
              conv_dw, conv_dw_b, conv_ln_g, conv_ln_b, w_conv_o, b_conv_o, w_out,
              norm_ffn_g, w_ffn_in, w_ffn_out, final_norm_g):
    b, s, _ = x.shape
    rope_tabs = axial_rope_tables(s)
    c_act = jax.nn.silu(c)
    split_at = (ATTN_WIDTH,
                ATTN_WIDTH + KV_WIDTH,
                ATTN_WIDTH + 2 * KV_WIDTH,
                ATTN_WIDTH + 2 * KV_WIDTH + CONV_WIDTH,
                ATTN_WIDTH + 2 * KV_WIDTH + 2 * CONV_WIDTH,
                ATTN_WIDTH + 2 * KV_WIDTH + 2 * CONV_WIDTH + D_MODEL)

    for l in range(DEPTH):
        mod = (c_act @ w_ada[l] + b_ada[l])[:, None, :]
        shift_m, scale_m, gate_m, shift_f, scale_f, gate_f = jnp.split(mod, 6, axis=-1)

        h = rms_norm(x, norm_mix_g[l]) * (1 + scale_m) + shift_m
        z = h @ w_in[l]
        zq, zk, zv, glu_a, glu_b, zg_conv, zg_attn = jnp.split(z, split_at, axis=-1)

        q = rms_norm(zq.reshape(b, s, N_Q_HEADS, HEAD_DIM), q_norm_g[l])
        k = rms_norm(zk.reshape(b, s, N_KV_HEADS, HEAD_DIM), k_norm_g[l])
        v = zv.reshape(b, s, N_KV_HEADS, HEAD_DIM)
        q = apply_axial_rope(q, rope_tabs)
        k = apply_axial_rope(k, rope_tabs)
        attn_out = blocked_gqa(q, k, v) @ w_attn_o[l]

        u = glu_a * jax.nn.sigmoid(glu_b)
        u = depthwise_conv(u, conv_dw[l], conv_dw_b[l])
        u = jax.nn.silu(layer_norm(u, conv_ln_g[l], conv_ln_b[l]))
        conv_out = u @ w_conv_o[l] + b_conv_o[l]

        merged = jax.nn.sigmoid(zg_conv) * conv_out + jax.nn.sigmoid(zg_attn) * attn_out
        x = x + gate_m * (merged @ w_out[l])

        h = rms_norm(x, norm_ffn_g[l]) * (1 + scale_f) + shift_f
        g_ff, u_ff = jnp.split(h @ w_ffn_in[l], 2, axis=-1)
        x = x + gate_f * ((jax.nn.silu(g_ff) * u_ff) @ w_ffn_out[l])

    return rms_norm(x, final_norm_g)
```

```python
import contextlib
import numpy as np
import concourse.bass as bass
import concourse.mybir as mybir
from concourse.bass_utils import run_bass_kernel_spmd

F32 = mybir.dt.float32
BF16 = mybir.dt.bfloat16
F32R = mybir.dt.float32r
AF = mybir.ActivationFunctionType
ALU = mybir.AluOpType

S = 2048
D = 1024
KC = 8
DEPTH = 4
DFF = 2816
INW = 3840
EPS = 1e-6
NCORES = 8

COMPUTE = ("pe", "act", "dve", "pool")

LP = 210
PK_C = 0
PK_L0 = 8
O_GMIX, O_GFFN, O_QG, O_KG, O_DW, O_DWB, O_LNG, O_LNB, O_BCO, O_BADA = 0, 8, 16, 17, 18, 142, 146, 150, 154, 162
NPK = PK_L0 + DEPTH * LP
C_ID, C_ROT, C_COS, C_SIN, C_EPS, C_ONE = 0, 128, 256, 2304, 4352, 4353
NCST = 4354
B_ONESD, B_BLK, B_ONESC, B_ID, B_SWAP, B_NID = 0, 128, 256, 384, 512, 640
NCSTB = 768


class View:
    __slots__ = ("ap", "regs")

    def __init__(self, ap, regs):
        self.ap = ap
        self.regs = regs


class Buf:
    def __init__(self, key, ap2d, esize, byte_off=0, excl=False):
        self.key = key
        self.ap = ap2d
        self.esize = esize
        self.off = byte_off
        self.excl = excl
        self.n = ap2d.shape[1]

    def _reg(self, c0, c1):
        lo = self.off + c0 * self.esize
        hi = self.off + c1 * self.esize
        if self.excl:
            lo = (lo // 2048) * 2048
            hi = ((hi + 2047) // 2048) * 2048
        return (self.key, lo, hi, self.excl)

    def v(self, c0=None, c1=None, p0=None, p1=None):
        c0 = 0 if c0 is None else c0
        c1 = self.n if c1 is None else c1
        assert 0 <= c0 < c1 <= self.n, (self.key, c0, c1, self.n)
        ap = self.ap[:, c0:c1] if p0 is None else self.ap[p0:p1, c0:c1]
        return View(ap, [self._reg(c0, c1)])

    def v3(self, c0, c1, a, p0=None, p1=None):
        vw = self.v(c0, c1, p0, p1)
        vw.ap = vw.ap.rearrange("p (a b) -> p a b", a=a)
        return vw

    def vs(self, a, t0, t1):
        b = self.n // a
        ap = self.ap.rearrange("p (a b) -> p a b", a=a)[:, :, t0:t1]
        return View(ap, [self._reg(i * b + t0, i * b + t1) for i in range(a)])


class Op:
    __slots__ = ("eng", "fn", "deps", "marked", "sig", "idx", "is_dma", "slot", "phase")


class Prog:
    def __init__(self, nc):
        self.nc = nc
        self.ops = {k: [] for k in ("pe", "act", "dve", "pool", "sp")}
        self.seg = {}
        self.allops = []
        self.phase = ''

    def _read(self, op, key, lo, hi, deps):
        for s in self.seg.get(key, ()):
            if s[0] < hi and lo < s[1]:
                if s[2] is not None:
                    deps.append(s[2])
                s[3][(op.eng, op.is_dma and id(op))] = op

    def _write(self, op, key, lo, hi, deps):
        segs = self.seg.get(key, [])
        new = []
        for s in segs:
            if s[0] < hi and lo < s[1]:
                if s[2] is not None:
                    deps.append(s[2])
                deps.extend(s[3].values())
                if s[0] < lo:
                    new.append([s[0], lo, s[2], dict(s[3])])
                if hi < s[1]:
                    new.append([hi, s[1], s[2], dict(s[3])])
            else:
                new.append(s)
        new.append([lo, hi, op, {}])
        self.seg[key] = new

    def op(self, eng, fn, r=(), w=(), dma=False, slot=None):
        o = Op()
        o.eng = eng
        o.fn = fn
        o.is_dma = dma
        o.slot = slot
        o.marked = False
        o.sig = None
        o.phase = self.phase
        deps = []
        for vw in r:
            for (key, lo, hi, excl) in vw.regs:
                if excl:
                    self._write(o, key, lo, hi, deps)
                else:
                    self._read(o, key, lo, hi, deps)
        for vw in w:
            for (key, lo, hi, excl) in vw.regs:
                self._write(o, key, lo, hi, deps)
        best = {}
        out = []
        for d in deps:
            if d is o:
                continue
            if d.is_dma:
                if d not in out:
                    out.append(d)
            else:
                if d.eng == "pe" and eng == "pe" and not dma:
                    continue
                b = best.get(d.eng)
                if b is None or d.idx > b.idx:
                    best[d.eng] = d
        out.extend(best.values())
        for d in out:
            d.marked = True
        o.deps = out
        lst = self.ops[eng]
        o.idx = len(lst)
        lst.append(o)
        self.allops.append(o)
        return o

    def emit(self, ctx, final_waits=()):
        nc = self.nc
        sems = {}
        counts = {}
        CH = 4000

        def getsem(name):
            if name not in sems:
                sems[name] = ctx.enter_context(nc.semaphore(name))
                counts[name] = 0
            return sems[name]

        for eng in COMPUTE:
            n = 0
            for o in self.ops[eng]:
                if o.is_dma:
                    continue
                if o.marked:
                    sname = f"s_{eng}_{n // CH}"
                    s = getsem(sname)
                    counts[sname] += 1
                    o.sig = (s, counts[sname], sname)
                    n += 1
        for o in self.allops:
            if o.is_dma:
                sname = f"d_{o.slot}"
                s = getsem(sname)
                counts[sname] += 16
                o.sig = (s, counts[sname], sname)
        self.nsems = len(sems)

        def run(eng, e):
            waited = {}
            for o in self.ops[eng]:
                for d in o.deps:
                    s, val, sname = d.sig
                    if waited.get(sname, 0) >= val:
                        continue
                    e.wait_ge(s, val)
                    waited[sname] = val
                ins = o.fn(e)
                if o.is_dma:
                    ins.then_inc(o.sig[0], 16)
                elif o.marked:
                    ins.then_inc(o.sig[0], 1)
            if eng == "sp":
                for o in final_waits:
                    s, val, sname = o.sig
                    e.wait_ge(s, val)

        with nc.Block() as block:
            @block.sync
            def _(e):
                run("sp", e)

            @block.gpsimd
            def _(e):
                run("pool", e)

            @block.scalar
            def _(e):
                run("act", e)

            @block.vector
            def _(e):
                run("dve", e)

            @block.tensor
            def _(e):
                run("pe", e)


def build(n_layers=DEPTH, dbg=None):
    nc = bass.Bass("TRN2", target_bir_lowering=False)
    dr = lambda name, shape: nc.dram_tensor(name, shape, F32, kind="ExternalInput").ap()
    x_d = dr("x", [S, D])
    pk_d = dr("pk", [128, NPK])
    cst_d = dr("cst", [128, NCST])
    cstb_d = dr("cstb", [128, NCSTB])
    fg_d = dr("fg", [128, D])
    w_ada_d = dr("w_ada", [DEPTH, D, 6 * D])
    w_in_d = dr("w_in", [DEPTH, D, INW])
    w_ao_d = dr("w_attn_o", [DEPTH, 512, D])
    w_co_d = dr("w_conv_o", [DEPTH, 512, D])
    w_out_d = dr("w_out", [DEPTH, D, D])
    w_fi_d = dr("w_ffn_in", [DEPTH, D, 2 * DFF])
    w_fo_d = dr("w_ffn_out", [DEPTH, DFF, D])
    out_d = nc.dram_tensor("out", [S, D], F32, kind="ExternalOutput").ap()
    dbg_d = {}
    if dbg:
        for name, shape in dbg.items():
            dbg_d[name] = nc.dram_tensor("dbg_" + name, shape, F32, kind="ExternalOutput").ap()

    ARENA_B = 68 * 1024
    RING_COLS = 10 * 1024
    with contextlib.ExitStack() as ctx:
        sb = lambda name, cols, dt: ctx.enter_context(nc.sbuf_tensor("sb_" + name, [128, cols], dt))
        xT_t = sb("xT", KC * S, F32)
        hT_t = sb("hT", KC * S, BF16)
        cst_t = sb("cstf", NCST, F32)
        cstb_t = sb("cstb", NCSTB, BF16)
        pk_t = sb("pk", NPK, F32)
        sm_t = sb("small", 256, F32)
        smb_t = sb("smallb", 16, BF16)
        ring_t = sb("ring", RING_COLS, BF16)
        arena_t = sb("arena", ARENA_B // 2, BF16)
        ps_t = ctx.enter_context(nc.psum_tensor("ps", [128, 4096], F32))

        P = Prog(nc)
        xT = Buf("xT", xT_t[:, :], 4)
        hT = Buf("hT", hT_t[:, :], 2)
        cst = Buf("cst", cst_t[:, :], 4)
        cstb = Buf("cstb", cstb_t[:, :], 2)
        pk = Buf("pk", pk_t[:, :], 4)
        sm = Buf("sm", sm_t[:, :], 4)
        smb = Buf("smb", smb_t[:, :], 2)
        ring = Buf("ring", ring_t[:, :], 2)
        ps = Buf("psum", ps_t[:, :], 4, 0, excl=True)

        def carve(off, ncols, dt):
            es = 4 if dt == F32 else 2
            assert off % 4 == 0 and off + ncols * es <= ARENA_B, (off, ncols, es)
            ap = arena_t[:, off // 2: off // 2 + ncols * es // 2]
            if dt == F32:
                ap = ap.bitcast(F32)
            return Buf("arena", ap, es, off)

        K1 = 1024
        cact = smb.v(0, 8)

        def mm(out, lhsT, rhs, start, stop):
            P.op("pe", lambda e: e.matmul(out.ap, lhsT=lhsT.ap, rhs=rhs.ap, start=start, stop=stop),
                 r=[lhsT, rhs], w=[out])

        def tr(out, in_, ident):
            P.op("pe", lambda e: e.transpose(out.ap, in_.ap, ident.ap), r=[in_, ident], w=[out])

        def act(out, in_, func, bias=None, scale=None, accum=None):
            r = [in_]
            kw = {}
            if bias is not None:
                kw["bias"] = bias.ap
                r.append(bias)
            if scale is not None:
                if isinstance(scale, View):
                    kw["scale"] = scale.ap
                    r.append(scale)
                else:
                    kw["scale"] = scale
            w = [out]
            if accum is not None:
                kw["accum_out"] = accum.ap
                w.append(accum)
            P.op("act", lambda e: e.activation(out=out.ap, in_=in_.ap, func=func, **kw), r=r, w=w)

        def tt(out, a, b, op, eng="dve"):
            P.op(eng, lambda e: e.tensor_tensor(out=out.ap, in0=a.ap, in1=b.ap, op=op), r=[a, b], w=[out])

        def stt(out, a, scalar, b, op0, op1):
            r = [a, b]
            if isinstance(scalar, View):
                r.append(scalar)
                sc = scalar.ap
            else:
                sc = scalar
            P.op("dve", lambda e: e.scalar_tensor_tensor(out=out.ap, in0=a.ap, scalar=sc, in1=b.ap, op0=op0, op1=op1),
                 r=r, w=[out])

        def tsm(out, a, scalar):
            r = [a]
            if isinstance(scalar, View):
                r.append(scalar)
                sc = scalar.ap
            else:
                sc = scalar
            P.op("dve", lambda e: e.tensor_scalar(out=out.ap, in0=a.ap, scalar1=sc, scalar2=None, op0=ALU.mult),
                 r=r, w=[out])

        def recip(out, a):
            P.op("dve", lambda e: e.reciprocal(out=out.ap, in_=a.ap), r=[a], w=[out])

        def cp(out, a, eng="dve"):
            P.op(eng, lambda e: e.tensor_copy(out=out.ap, in_=a.ap), r=[a], w=[out])

        def memset(out, val, eng="dve"):
            P.op(eng, lambda e: e.memset(out.ap, val), w=[out])

        def dma(eng, out, in_ap, slot, r=()):
            return P.op(eng, lambda e: e.dma_start(out=out.ap, in_=in_ap), r=list(r), w=[out], dma=True, slot=slot)

        def dma_out(eng, out_ap, in_, slot):
            return P.op(eng, lambda e: e.dma_start(out=out_ap, in_=in_.ap), r=[in_], dma=True, slot=slot)

        rstate = {"pos": 0, "n": 0, "lo": 0, "hi": RING_COLS}

        def wslab(src_ap, kcn, ncols):
            tot = kcn * ncols
            if rstate["pos"] + tot > rstate["hi"] or rstate["pos"] < rstate["lo"]:
                rstate["pos"] = rstate["lo"]
            c0 = rstate["pos"]
            rstate["pos"] += tot
            i = rstate["n"]
            rstate["n"] += 1
            dst = ring.v3(c0, c0 + tot, kcn)
            dma("pool", dst, src_ap.rearrange("(kc p) n -> p kc n", p=128), f"w{i % 16}")
            return lambda kc, a, b: ring.v(c0 + kc * ncols + a, c0 + kc * ncols + b)

        bank = lambda i, a=0, b=512: ps.v(i * 512 + a, i * 512 + b)
        pair = lambda i, a=0, b=1024: ps.v(i * 1024 + a, i * 1024 + b)
        grp = lambda i, a=0, b=2048: ps.v(i * 2048 + a, i * 2048 + b)

        SCR = 32 * 1024

        dma("sp", pk.v(), pk_d, "pk")
        dma("sp", cst.v(), cst_d, "cst")
        dma("pool", cstb.v(), cstb_d, "cstb")
        ident = cst.v(C_ID, C_ID + 128)
        rotT = cst.v(C_ROT, C_ROT + 128)
        epsc = cst.v(C_EPS, C_EPS + 1)
        onesd = cstb.v(B_ONESD, B_ONESD + 128)
        blk64 = cstb.v(B_BLK, B_BLK + 128)
        onesc = cstb.v(B_ONESC, B_ONESC + 128)
        identb = cstb.v(B_ID, B_ID + 128)
        swapb = cstb.v(B_SWAP, B_SWAP + 128)
        nidentb = cstb.v(B_NID, B_NID + 128)

        P.phase = 'xload'
        for t in range(16):
            stg = carve(SCR + (t % 4) * 4096, 1024, F32)
            dma("sp", stg.v(), x_d[t * 128:(t + 1) * 128, :], f"xs{t % 4}")
            pr = t % 4
            for kc in range(KC):
                tr(pair(pr, kc * 128, kc * 128 + 128), stg.v(kc * 128, kc * 128 + 128), ident)
            src = pair(pr)
            src.ap = src.ap.rearrange("p (a b) -> p a b", a=KC)
            dstv = xT.vs(KC, t * 128, (t + 1) * 128)
            if t % 2 == 0:
                P.op("act", lambda e, o=dstv, i=src: e.activation(out=o.ap, in_=i.ap, func=AF.Copy), r=[src], w=[dstv])
            else:
                cp(dstv, src)

        act(cact, pk.v(PK_C, PK_C + 8), AF.Silu)

        def norm_bufs(sq_off, rstd_off, tmp_offs):
            return dict(sqs=[carve(sq_off + i * 4096, S, BF16) for i in range(2)],
                        rstd=carve(rstd_off, S, F32), tmps=[carve(o, S, F32) for o in tmp_offs])

        def norm_chunk(nb, kc):
            sq = nb["sqs"][kc % 2]
            act(sq.v(), xT.v(kc * S, (kc + 1) * S), AF.Square)
            for tb in range(4):
                mm(bank(4 + tb), onesd, sq.v(tb * 512, tb * 512 + 512), kc == 0, kc == KC - 1)

        def norm_finish(nb, gs_col, shift_col):
            rstd = nb["rstd"]
            act(rstd.v(), grp(1), AF.Ln, bias=epsc)
            act(rstd.v(), rstd.v(), AF.Exp, scale=-0.5)
            for kc in range(KC):
                tmp = nb["tmps"][kc % 2]
                tt(tmp.v(), xT.v(kc * S, (kc + 1) * S), rstd.v(), ALU.mult)
                act(hT.v(kc * S, (kc + 1) * S), tmp.v(), AF.Identity,
                    bias=sm.v(shift_col + kc, shift_col + kc + 1), scale=sm.v(gs_col + kc, gs_col + kc + 1))

        NB2 = lambda: norm_bufs(SCR, SCR + 8192, [SCR + 16384, SCR + 24576])
        NB1 = lambda: norm_bufs(49152, 57344, [0, 8192])

        def proj(out_fn, slab, ncol0, kcn, rhs_fn, tbs=range(4), t0=0):
            for kc in range(kcn):
                w = slab(kc, ncol0, ncol0 + 128)
                for tb in tbs:
                    mm(out_fn(tb), w, rhs_fn(kc, tb), kc == 0, kc == kcn - 1)

        hT_rhs = lambda kc, tb: hT.v(kc * S + tb * 512, kc * S + tb * 512 + 512)

        def mod_slab_mm(l_, s_, cb=0):
            slab = wslab(w_ada_d[l_, :, s_ * 256:(s_ + 1) * 256], KC, 256)

            def half(h):
                j = s_ * 2 + h
                for kc in range(KC):
                    mm(bank(6, cb + j, cb + j + 1), slab(kc, h * 128, h * 128 + 128), smb.v(kc, kc + 1), kc == 0, kc == KC - 1)
            return [lambda: half(0), lambda: half(1)]

        def mod_finish(l_, mb_, c0=0, c1=48, cb=0):
            o_ = PK_L0 + l_ * LP
            tt(sm.v(mb_ + c0, mb_ + c1), bank(6, cb + c0, cb + c1), pk.v(o_ + O_BADA + c0, o_ + O_BADA + c1), ALU.add)
            if c0 == 0:
                stt(sm.v(mb_ + 48, mb_ + 56), sm.v(mb_ + 8, mb_ + 16), 1.0, pk.v(o_ + O_GMIX, o_ + O_GMIX + 8), ALU.add, ALU.mult)
                tsm(sm.v(mb_ + 64, mb_ + 65), pk.v(o_ + O_QG, o_ + O_QG + 1), 0.125)
            if c1 == 48:
                stt(sm.v(mb_ + 56, mb_ + 64), sm.v(mb_ + 32, mb_ + 40), 1.0, pk.v(o_ + O_GFFN, o_ + O_GFFN + 8), ALU.add, ALU.mult)

        for l in range(n_layers):
            pkl = PK_L0 + l * LP
            pcol = lambda o, n=1: pk.v(pkl + o, pkl + o + n)

            mb = (l % 2) * 80
            modT = lambda a, b=None, mb=mb: sm.v(mb + a, mb + (a + 1 if b is None else b))
            if l == 0:
                P.phase = 'L0.mod'
                for s_ in range(8):
                    for f_ in mod_slab_mm(0, s_):
                        f_()
                mod_finish(0, mb, 0, 16)
            SH_M, GATE_M, SH_F, GATE_F = 0, 16, 24, 40

            P.phase = f'L{l}.norm1'
            if l == 0:
                nb1 = NB1()
                for kc in range(KC):
                    norm_chunk(nb1, kc)
            norm_finish(nb1, mb + 48, mb + SH_M)

            attnT = carve(0, 4 * S, BF16)
            ucn = carve(16384, 4 * S, BF16)
            kTa = carve(16384, S, BF16)
            kTb = carve(20480, S, BF16)
            qTs = [carve(24576 + i * 4096, S, BF16) for i in range(2)]
            Vaug = carve(SCR, 16 * 2 * 192, BF16)
            Es = [carve(SCR + 12288 + i * 2048, K1, BF16) for i in range(3)]
            sqb = carve(SCR + 18432, K1, BF16)
            rstd_q = carve(SCR + 20480, K1, F32)
            qn = carve(SCR + 24576, K1, F32)
            t1 = carve(SCR + 28672, K1, F32)
            rden = carve(SCR + 32768, K1, F32)

            sqbs = [sqb, carve(SCR + 12288, K1, BF16)]
            rstds = [rstd_q, carve(SCR + 14336, K1, F32)]
            qns = [qn, rden]
            qbuf = [carve((c + 1) * 4096, S, BF16) for c in range(3)] + [qTs[0]]

            def qk_stages(job, slab, ncol0, gcol, dst, half):
                b_ = job % 2
                zp = pair(b_)
                sq_, rs_, qn_ = sqbs[b_], rstds[b_], qns[b_]
                c0 = half * K1

                def sA():
                    proj(lambda tb: pair(b_, tb * 512, tb * 512 + 512), slab, ncol0, KC,
                         lambda kc, tb: hT_rhs(kc, half * 2 + tb), tbs=range(2))

                def sB():
                    act(sq_.v(), pair(b_), AF.Square)

                def sC():
                    for i in range(2):
                        mm(pair(2, i * 512, i * 512 + 512), blk64, sq_.v(i * 512, i * 512 + 512), True, True)

                def sD():
                    act(rs_.v(), pair(2), AF.Ln, bias=epsc)
                    act(rs_.v(), rs_.v(), AF.Exp, scale=-0.5)

                def sE():
                    stt(qn_.v(), pair(b_), gcol, rs_.v(), ALU.mult, ALU.mult)

                def sF():
                    for i in range(2):
                        mm(pair(3, i * 512, i * 512 + 512), rotT, qn_.v(i * 512, i * 512 + 512), True, True)

                def sG():
                    tt(t1.v(), qn_.v(), cst.v(C_COS + c0, C_COS + c0 + K1), ALU.mult)
                    tt(qn_.v(), pair(3), cst.v(C_SIN + c0, C_SIN + c0 + K1), ALU.mult)
                    tt(dst.v(c0, c0 + K1), t1.v(), qn_.v(), ALU.add)
                return [sA, sB, sC, sD, sE, sF, sG]

            P.phase = f'L{l}.kv'
            rstate["pos"] = 0
            slab_kv = wslab(w_in_d[l, :, 512:768], KC, 256)
            slab_q = [wslab(w_in_d[l, :, i * 256:(i + 1) * 256], KC, 256) for i in range(2)]
            rstate["lo"] = rstate["pos"]
            assert rstate["lo"] == 6144
            memset(Vaug.v(), 1.0)
            for t in range(16):
                for kc in range(KC):
                    mm(ps.v(2048 + t * 128, 2048 + t * 128 + 128), hT.v(kc * S + t * 128, kc * S + t * 128 + 128),
                       slab_kv(kc, 128, 256), kc == 0, kc == KC - 1)
            vsrc = grp(1)
            vsrc.ap = vsrc.ap.rearrange("p (t g c) -> p t g c", t=16, g=2)
            vdst = Vaug.v()
            vdst.ap = vdst.ap.rearrange("p (t g c) -> p t g c", t=16, g=2)[:, :, :, 64:128]
            cp(vdst, vsrc)
            jobs = []
            for half in range(2):
                jobs.append(qk_stages(len(jobs), slab_kv, 0, pcol(O_KG), kTa, half))
            for c in range(4):
                for half in range(2):
                    jobs.append(qk_stages(len(jobs), slab_q[c // 2], (c % 2) * 128, sm.v(mb + 64, mb + 65), qbuf[c], half))
            SKEW = 4
            for t in range(len(jobs) * SKEW + 7):
                for j, job in enumerate(jobs):
                    st_ = t - j * SKEW
                    if 0 <= st_ < 7:
                        job[st_]()
                if t == 1 * SKEW + 7:
                    pass
            Z11 = carve(SCR + 24576, S, BF16)
            Z01 = carve(SCR + 28672, S, BF16)
            for tb in range(4):
                mm(bank(tb), swapb, kTa.v(tb * 512, tb * 512 + 512), True, True)
            cp(Z11.v(0, S, 64, 128), kTa.v(0, S, 64, 128))
            memset(Z11.v(0, S, 0, 64), 0.0)
            memset(Z01.v(0, S, 0, 64), 0.0)
            act(kTb.v(0, S, 0, 64), ps.v(0, 2048, 0, 64), AF.Copy)
            act(Z01.v(0, S, 64, 128), ps.v(0, 2048, 64, 128), AF.Copy)
            memset(kTa.v(0, S, 64, 128), 0.0)
            memset(kTb.v(0, S, 64, 128), 0.0)
            kz = [[kTa, Z01], [kTb, Z11]]

            P.phase = f'L{l}.attn'
            accsb = rstd_q
            mods_pending = [(l + 1, s_, 0) for s_ in range(24)] if l + 1 < n_layers else []
            if l == 0:
                mods_pending = [(0, s_, 64) for s_ in range(8, 24)] + mods_pending
            mod_every = max(2, 256 // (len(mods_pending) + 1))
            mod_halves = []
            gstep = 0

            def s_emit(qT, g, st, idx):
                par, qh, kb = st
                kX = kz[g][par]
                for i in range(2):
                    mm(pair(idx % 2, i * 512, i * 512 + 512),
                       kX.v(kb * 128, kb * 128 + 128),
                       qT.v(qh * K1 + i * 512, qh * K1 + i * 512 + 512), True, True)

            steps = [(c, par, qh, kb) for c in range(4) for par in range(2) for qh in range(2) for kb in range(16)]
            NS = len(steps)
            s_emit(qbuf[steps[0][0]], steps[0][0] // 2, steps[0][1:], 0)
            s_emit(qbuf[steps[1][0]], steps[1][0] // 2, steps[1][1:], 1)
            for si, (c, par, qh, kb) in enumerate(steps):
                g = c // 2
                E = Es[si % 3]
                act(E.v(), pair(si % 2), AF.Exp)
                if si + 2 < NS:
                    c2 = steps[si + 2][0]
                    s_emit(qbuf[c2], c2 // 2, steps[si + 2][1:], si + 2)
                vlo = 0 if par == 1 else 64
                vb = (kb * 2 + g) * 192 + vlo
                for i in range(2):
                    mm(pair(2, i * 512, i * 512 + 512), Vaug.v(vb, vb + 128),
                       E.v(i * 512, i * 512 + 512), kb == 0, kb == 15)
                if si < 8:
                    mm(bank(7), identb, hT.v(0, 512), True, True)
                if kb == 15:
                    orow0, drow0 = (0, 64) if par == 0 else (64, 0)
                    cp(accsb.v(0, 512), bank(4))
                    cp(accsb.v(512, K1), bank(5))
                    rd = rden.v(0, K1, orow0, orow0 + 64)
                    recip(rd, accsb.v(0, K1, drow0, drow0 + 64))
                    tt(attnT.v(c * S + qh * K1, c * S + qh * K1 + K1, orow0, orow0 + 64),
                       accsb.v(0, K1, orow0, orow0 + 64), rd, ALU.mult)
                gstep += 1
                if mod_halves:
                    mod_halves.pop(0)()
                elif mods_pending and gstep % mod_every == mod_every // 2:
                    mod_halves = mod_slab_mm(*mods_pending.pop(0))
                    mod_halves.pop(0)()
            while mods_pending or mod_halves:
                if not mod_halves:
                    mod_halves = mod_slab_mm(*mods_pending.pop(0))
                mod_halves.pop(0)()
            if l == 0:
                mod_finish(0, mb, 16, 48, 64)
            if l + 1 < n_layers:
                mod_finish(l + 1, ((l + 1) % 2) * 80)
            rstate["lo"] = 0

            P.phase = f'L{l}.conv'
            sbuf_ = carve(SCR, S, BF16)
            UW = S + 32
            us = [carve(SCR + 4096 + i * (UW * 2), UW, BF16) for i in range(2)]
            diags = [carve(SCR + 4096 + 2 * UW * 2 + i * 7936, 31 * 128, BF16) for i in range(2)]
            TD = 9
            cacc = carve(SCR + 4096 + 2 * UW * 2 + 2 * 7936, S, F32)
            slab_a = [None, None]
            slab_b = [None, None]
            for c in range(4):
                if c % 2 == 0:
                    slab_a[0] = wslab(w_in_d[l, :, 768 + (c // 2) * 256: 768 + (c // 2) * 256 + 256], KC, 256)
                    slab_b[0] = wslab(w_in_d[l, :, 1280 + (c // 2) * 256: 1280 + (c // 2) * 256 + 256], KC, 256)
                u = us[c % 2]
                dg = diags[c % 2]
                for j in range(TD, 31):
                    tsm(dg.v(j * 128, j * 128 + 128), identb, pcol(O_DW + c * 31 + j))
                memset(u.v(0, 16), 0.0)
                memset(u.v(16 + S, UW), 0.0)
                proj(lambda tb: bank(4 + tb), slab_b[0], (c % 2) * 128, KC, hT_rhs)
                for th in range(2):
                    act(sbuf_.v(th * K1, th * K1 + K1), pair(2 + th), AF.Sigmoid)
                proj(lambda tb: bank(tb), slab_a[0], (c % 2) * 128, KC, hT_rhs)
                for th in range(2):
                    tt(u.v(16 + th * K1, 16 + th * K1 + K1), pair(th), sbuf_.v(th * K1, th * K1 + K1), ALU.mult)
                for j in range(TD):
                    src = u.v(j + 1, j + 1 + S)
                    wj = pcol(O_DW + c * 31 + j)
                    if j == 0:
                        tsm(cacc.v(), src, wj)
                    else:
                        stt(cacc.v(), src, wj, cacc.v(), ALU.mult, ALU.add)
                for tb in range(4):
                    for j in range(TD, 31):
                        mm(bank(4 + tb), dg.v(j * 128, j * 128 + 128), u.v(tb * 512 + j + 1, tb * 512 + j + 1 + 512), j == TD, j == 30)
                stt(ucn.v(c * S, (c + 1) * S), grp(1), pcol(O_DWB + c), cacc.v(), ALU.add, ALU.add)
            P.phase = f'L{l}.ln'
            sqs = [carve(SCR + i * 4096, S, BF16) for i in range(2)]
            m2 = carve(SCR + 8192, S, F32)
            tmps = [carve(SCR + 16384 + i * 8192, S, F32) for i in range(2)]
            for c in range(4):
                sq = sqs[c % 2]
                act(sq.v(), ucn.v(c * S, (c + 1) * S), AF.Square)
                for tb in range(4):
                    mm(bank(tb), onesc, ucn.v(c * S + tb * 512, c * S + tb * 512 + 512), c == 0, c == 3)
                for tb in range(4):
                    mm(bank(4 + tb), onesc, sq.v(tb * 512, tb * 512 + 512), c == 0, c == 3)
            act(m2.v(), grp(0), AF.Square)
            meanb = sqs[0]
            act(meanb.v(), grp(0), AF.Copy)
            tt(m2.v(), grp(1), m2.v(), ALU.subtract)
            act(m2.v(), m2.v(), AF.Ln, bias=epsc)
            act(m2.v(), m2.v(), AF.Exp, scale=-0.5)
            for c in range(4):
                gi_ = (c + 1) % 2
                for tb in range(4):
                    mm(bank(gi_ * 4 + tb), identb, ucn.v(c * S + tb * 512, c * S + tb * 512 + 512), True, False)
                    mm(bank(gi_ * 4 + tb), nidentb, meanb.v(tb * 512, tb * 512 + 512), False, True)
                tmp = tmps[c % 2]
                tt(tmp.v(), grp(gi_), m2.v(), ALU.mult)
                act(ucn.v(c * S, (c + 1) * S), tmp.v(), AF.Silu, bias=pcol(O_LNB + c), scale=pcol(O_LNG + c))

            P.phase = f'L{l}.merge'
            sgc = carve(SCR, S, BF16)
            sga = carve(SCR + 4096, S, BF16)
            mbuf = carve(SCR + 8192, S, BF16)
            tbuf = carve(SCR + 12288, S, F32)
            merged = carve(SCR + 20480, 4 * S, BF16)
            gcount = 0
            for hj in range(2):
                for jj in range(4):
                    j = hj * 4 + jj
                    if j % 2 == 0:
                        s_gc = wslab(w_in_d[l, :, 1792 + (j // 2) * 256: 1792 + (j // 2) * 256 + 256], KC, 256)
                        s_co = wslab(w_co_d[l, :, (j // 2) * 256:(j // 2) * 256 + 256], 4, 256)
                        s_ga = wslab(w_in_d[l, :, 2816 + (j // 2) * 256: 2816 + (j // 2) * 256 + 256], KC, 256)
                        s_ao = wslab(w_ao_d[l, :, (j // 2) * 256:(j // 2) * 256 + 256], 4, 256)
                    n0 = (j % 2) * 128
                    proj(lambda tb: bank(4 + tb), s_gc, n0, KC, hT_rhs)
                    act(sgc.v(), grp(1), AF.Sigmoid)
                    proj(lambda tb: bank(tb), s_co, n0, 4,
                         lambda kc, tb: ucn.v(kc * S + tb * 512, kc * S + tb * 512 + 512))
                    stt(mbuf.v(), grp(0), pcol(O_BCO + j), sgc.v(), ALU.add, ALU.mult)
                    proj(lambda tb: bank(4 + tb), s_ga, n0, KC, hT_rhs)
                    act(sga.v(), grp(1), AF.Sigmoid)
                    proj(lambda tb: bank(tb), s_ao, n0, 4,
                         lambda kc, tb: attnT.v(kc * S + tb * 512, kc * S + tb * 512 + 512))
                    tt(tbuf.v(), grp(0), sga.v(), ALU.mult)
                    tt(merged.v(jj * S, (jj + 1) * S), tbuf.v(), mbuf.v(), ALU.add)
                if hj == 1:
                    nb2 = NB2()
                for n in range(8):
                    if n % 2 == 0:
                        s_wo = wslab(w_out_d[l, hj * 512:(hj + 1) * 512, (n // 2) * 256:(n // 2) * 256 + 256], 4, 256)
                    if hj == 0:
                        gi = gcount % 2
                        gcount += 1
                        proj(lambda tb: bank(gi * 4 + tb), s_wo, (n % 2) * 128, 4,
                             lambda kc, tb: merged.v(kc * S + tb * 512, kc * S + tb * 512 + 512))
                        stt(xT.v(n * S, (n + 1) * S), grp(gi), modT(GATE_M + n), xT.v(n * S, (n + 1) * S), ALU.mult, ALU.add)
                    else:
                        for th in range(2):
                            proj(lambda tb: pair(th, tb * 512, tb * 512 + 512), s_wo, (n % 2) * 128, 4,
                                 lambda kc, tb: merged.v(kc * S + (th * 2 + tb) * 512, kc * S + (th * 2 + tb) * 512 + 512),
                                 tbs=range(2))
                            xs_ = xT.v(n * S + th * K1, n * S + th * K1 + K1)
                            stt(xs_, pair(th), modT(GATE_M + n), xs_, ALU.mult, ALU.add)
                        if n >= 1:
                            norm_chunk(nb2, n - 1)
                if hj == 1:
                    norm_chunk(nb2, 7)

            P.phase = f'L{l}.norm2'
            norm_finish(nb2, mb + 56, mb + SH_F)
            P.phase = f'L{l}.ffn'
            actb = carve(0, 12 * S, BF16)
            sgs = [carve(12 * S * 2 + i * 4096, S, BF16) for i in range(2)]
            f0 = 0
            for hf, nf in enumerate((12, 10)):
                for ff in range(nf):
                    f = f0 + ff
                    if f % 2 == 0:
                        s_g = wslab(w_fi_d[l, :, (f // 2) * 256:(f // 2) * 256 + 256], KC, 256)
                        s_u = wslab(w_fi_d[l, :, DFF + (f // 2) * 256: DFF + (f // 2) * 256 + 256], KC, 256)
                    sg = sgs[f % 2]
                    proj(lambda tb: bank(tb), s_g, (f % 2) * 128, KC, hT_rhs)
                    act(sg.v(), grp(0), AF.Silu)
                    proj(lambda tb: bank(4 + tb), s_u, (f % 2) * 128, KC, hT_rhs)
                    tt(actb.v(ff * S, (ff + 1) * S), grp(1), sg.v(), ALU.mult)
                ovl = (hf == 1 and l + 1 < n_layers)
                if ovl:
                    nb1 = NB1()
                for n in range(8):
                    s_fo = wslab(w_fo_d[l, f0 * 128:(f0 + nf) * 128, n * 128:(n + 1) * 128], nf, 128)
                    if not ovl:
                        gi = gcount % 2
                        gcount += 1
                        proj(lambda tb: bank(gi * 4 + tb), s_fo, 0, nf,
                             lambda kc, tb: actb.v(kc * S + tb * 512, kc * S + tb * 512 + 512))
                        stt(xT.v(n * S, (n + 1) * S), grp(gi), modT(GATE_F + n), xT.v(n * S, (n + 1) * S), ALU.mult, ALU.add)
                    else:
                        for th in range(2):
                            proj(lambda tb: pair(th, tb * 512, tb * 512 + 512), s_fo, 0, nf,
                                 lambda kc, tb: actb.v(kc * S + (th * 2 + tb) * 512, kc * S + (th * 2 + tb) * 512 + 512),
                                 tbs=range(2))
                            xs_ = xT.v(n * S + th * K1, n * S + th * K1 + K1)
                            stt(xs_, pair(th), modT(GATE_F + n), xs_, ALU.mult, ALU.add)
                        if n >= 1:
                            norm_chunk(nb1, n - 1)
                if ovl:
                    norm_chunk(nb1, 7)
                f0 += nf

        P.phase = 'final'
        fg = carve(0, D, F32)
        dma("sp", fg.v(), fg_d, "fg")
        junk = carve(4096, D, F32)
        ostg = [carve(8192 + i * 4096, D, F32) for i in range(2)]
        stat = carve(16384, 64, F32)
        memset(stat.v(), 0.0)
        outs = []
        for t in range(16):
            pr = t % 4
            for kc in range(KC):
                tr(pair(pr, kc * 128, kc * 128 + 128), xT.v(kc * S + t * 128, kc * S + t * 128 + 128), ident)
            ss = stat.v(2 * t, 2 * t + 1)
            sd = stat.v(2 * t + 1, 2 * t + 2)
            act(junk.v(), pair(pr), AF.Square, accum=ss)
            act(sd, ss, AF.Sqrt, bias=epsc, scale=1.0 / D)
            recip(sd, sd)
            o = ostg[t % 2]
            stt(o.v(), pair(pr), sd, fg.v(), ALU.mult, ALU.mult)
            outs.append(dma_out("sp", out_d[t * 128:(t + 1) * 128, :], o.v(), f"o{t % 2}"))
        if dbg and "sm" in dbg_d:
            outs.append(dma_out("sp", dbg_d["sm"], sm.v(), "dbgsm"))
            outs = outs[-3:]
        else:
            outs = outs[-2:]
        P.emit(ctx, final_waits=outs)
        build.stats = {k: len(v) for k, v in P.ops.items()}
        build.stats["sems"] = P.nsems
        build.prog = P
    return nc


def _consts():
    cst = np.zeros((128, NCST), np.float32)
    cst[:, C_ID:C_ID + 128] = np.eye(128, dtype=np.float32)
    R = np.zeros((128, 128), np.float32)
    for p in range(128):
        i = p % 32
        if i < 16:
            R[p, p + 16] = -1.0
        else:
            R[p, p - 16] = 1.0
    cst[:, C_ROT:C_ROT + 128] = R.T
    tpos = np.arange(S)
    row = (tpos // 64).astype(np.float32)
    col = (tpos % 64).astype(np.float32)
    inv = (np.float32(10000.0) ** (-np.arange(0, 32, 2, dtype=np.float32) / np.float32(32))).astype(np.float32)
    for p in range(128):
        i = p % 16
        seg = (p % 64) // 32
        pos = row if seg == 0 else col
        ang = (pos * inv[i]).astype(np.float32)
        cst[p, C_COS:C_COS + S] = np.cos(ang)
        cst[p, C_SIN:C_SIN + S] = np.sin(ang)
    cst[:, C_EPS] = EPS
    cst[:, C_ONE] = 1.0
    cb = np.zeros((128, NCSTB), np.float32)
    cb[:, B_ONESD:B_ONESD + 128] = 1.0 / D
    blk = np.zeros((128, 128), np.float32)
    blk[:64, :64] = 1.0 / 64
    blk[64:, 64:] = 1.0 / 64
    cb[:, B_BLK:B_BLK + 128] = blk
    cb[:, B_ONESC:B_ONESC + 128] = 1.0 / 512
    cb[:, B_ID:B_ID + 128] = np.eye(128, dtype=np.float32)
    sw = np.zeros((128, 128), np.float32)
    for p in range(128):
        sw[(p + 64) % 128, p] = 1.0
    cb[:, B_SWAP:B_SWAP + 128] = sw
    cb[:, B_NID:B_NID + 128] = -np.eye(128, dtype=np.float32)
    return cst, cb


def _pack(b, c, b_ada, norm_mix_g, q_norm_g, k_norm_g, conv_dw, conv_dw_b, conv_ln_g, conv_ln_b,
          b_conv_o, norm_ffn_g):
    pk = np.zeros((128, NPK), np.float32)
    col = lambda v: np.ascontiguousarray(v.reshape(-1, 128).T)
    pk[:, PK_C:PK_C + 8] = col(c[b])
    for l in range(DEPTH):
        o = PK_L0 + l * LP
        pk[:, o + O_GMIX:o + O_GMIX + 8] = col(norm_mix_g[l])
        pk[:, o + O_GFFN:o + O_GFFN + 8] = col(norm_ffn_g[l])
        pk[:, o + O_QG] = np.tile(q_norm_g[l], 2)
        pk[:, o + O_KG] = np.tile(k_norm_g[l], 2)
        dw = conv_dw[l].reshape(31, 4, 128).transpose(2, 1, 0).reshape(128, 124)
        pk[:, o + O_DW:o + O_DW + 124] = dw
        pk[:, o + O_DWB:o + O_DWB + 4] = col(conv_dw_b[l])
        pk[:, o + O_LNG:o + O_LNG + 4] = col(conv_ln_g[l])
        pk[:, o + O_LNB:o + O_LNB + 4] = col(conv_ln_b[l])
        pk[:, o + O_BCO:o + O_BCO + 8] = col(b_conv_o[l])
        pk[:, o + O_BADA:o + O_BADA + 48] = col(b_ada[l])
    return pk


_NC_CACHE = {}


def make_in_maps(cores, x, c, w_ada, b_ada, norm_mix_g, w_in, q_norm_g, k_norm_g, w_attn_o,
                 conv_dw, conv_dw_b, conv_ln_g, conv_ln_b, w_conv_o, b_conv_o, w_out,
                 norm_ffn_g, w_ffn_in, w_ffn_out, final_norm_g):
    f = lambda a: np.ascontiguousarray(np.asarray(a, dtype=np.float32))
    cst, cb = _consts()
    fg = np.ascontiguousarray(np.broadcast_to(f(final_norm_g)[None, :], (128, D)))
    shared = {"cst": cst, "cstb": cb, "fg": fg, "w_ada": f(w_ada), "w_in": f(w_in), "w_attn_o": f(w_attn_o),
              "w_conv_o": f(w_conv_o), "w_out": f(w_out), "w_ffn_in": f(w_ffn_in), "w_ffn_out": f(w_ffn_out)}
    x = f(x)
    maps = []
    for b in cores:
        m = dict(shared)
        m["x"] = np.ascontiguousarray(x[b])
        m["pk"] = _pack(b, f(c), f(b_ada), f(norm_mix_g), f(q_norm_g), f(k_norm_g), f(conv_dw), f(conv_dw_b),
                        f(conv_ln_g), f(conv_ln_b), f(b_conv_o), f(norm_ffn_g))
        maps.append(m)
    return maps


def kernel(**inputs):
    if "nc" not in _NC_CACHE:
        _NC_CACHE["nc"] = build(DEPTH)
    nc = _NC_CACHE["nc"]
    maps = make_in_maps(list(range(NCORES)), **inputs)
    res = run_bass_kernel_spmd(nc, maps, core_ids=list(range(NCORES)))
    out = np.stack([np.asarray(r["out"], dtype=np.float32) for r in res.results], axis=0)
    return out
```

```python
import contextlib
import numpy as np
import concourse.bass as bass
import concourse.mybir as mybir
from concourse.bass_utils import run_bass_kernel_spmd

F32 = mybir.dt.float32
BF16 = mybir.dt.bfloat16
F32R = mybir.dt.float32r
AF = mybir.ActivationFunctionType
ALU = mybir.AluOpType

S = 2048
D = 1024
KC = 8
DEPTH = 4
DFF = 2816
INW = 3840
EPS = 1e-6
NCORES = 8

COMPUTE = ("pe", "act", "dve", "pool")

LP = 210
PK_C = 0
PK_L0 = 8
O_GMIX, O_GFFN, O_QG, O_KG, O_DW, O_DWB, O_LNG, O_LNB, O_BCO, O_BADA = 0, 8, 16, 17, 18, 142, 146, 150, 154, 162
NPK = PK_L0 + DEPTH * LP
C_ID, C_ROT, C_COS, C_SIN, C_EPS, C_ONE = 0, 128, 256, 2304, 4352, 4353
NCST = 4354
B_ONESD, B_BLK, B_ONESC, B_ID, B_SWAP, B_NID = 0, 128, 256, 384, 512, 640
NCSTB = 768


class View:
    __slots__ = ("ap", "regs")

    def __init__(self, ap, regs):
        self.ap = ap
        self.regs = regs


class Buf:
    def __init__(self, key, ap2d, esize, byte_off=0, excl=False):
        self.key = key
        self.ap = ap2d
        self.esize = esize
        self.off = byte_off
        self.excl = excl
        self.n = ap2d.shape[1]

    def _reg(self, c0, c1):
        lo = self.off + c0 * self.esize
        hi = self.off + c1 * self.esize
        if self.excl:
            lo = (lo // 2048) * 2048
            hi = ((hi + 2047) // 2048) * 2048
        return (self.key, lo, hi, self.excl)

    def v(self, c0=None, c1=None, p0=None, p1=None):
        c0 = 0 if c0 is None else c0
        c1 = self.n if c1 is None else c1
        assert 0 <= c0 < c1 <= self.n, (self.key, c0, c1, self.n)
        ap = self.ap[:, c0:c1] if p0 is None else self.ap[p0:p1, c0:c1]
        return View(ap, [self._reg(c0, c1)])

    def v3(self, c0, c1, a, p0=None, p1=None):
        vw = self.v(c0, c1, p0, p1)
        vw.ap = vw.ap.rearrange("p (a b) -> p a b", a=a)
        return vw

    def vs(self, a, t0, t1):
        b = self.n // a
        ap = self.ap.rearrange("p (a b) -> p a b", a=a)[:, :, t0:t1]
        return View(ap, [self._reg(i * b + t0, i * b + t1) for i in range(a)])


class Op:
    __slots__ = ("eng", "fn", "deps", "marked", "sig", "idx", "is_dma", "slot", "phase")


class Prog:
    def __init__(self, nc):
        self.nc = nc
        self.ops = {k: [] for k in ("pe", "act", "dve", "pool", "sp")}
        self.seg = {}
        self.allops = []
        self.phase = ''

    def _read(self, op, key, lo, hi, deps):
        for s in self.seg.get(key, ()):
            if s[0] < hi and lo < s[1]:
                if s[2] is not None:
                    deps.append(s[2])
                s[3][(op.eng, op.is_dma and id(op))] = op

    def _write(self, op, key, lo, hi, deps):
        segs = self.seg.get(key, [])
        new = []
        for s in segs:
            if s[0] < hi and lo < s[1]:
                if s[2] is not None:
                    deps.append(s[2])
                deps.extend(s[3].values())
                if s[0] < lo:
                    new.append([s[0], lo, s[2], dict(s[3])])
                if hi < s[1]:
                    new.append([hi, s[1], s[2], dict(s[3])])
            else:
                new.append(s)
        new.append([lo, hi, op, {}])
        self.seg[key] = new

    def op(self, eng, fn, r=(), w=(), dma=False, slot=None):
        o = Op()
        o.eng = eng
        o.fn = fn
        o.is_dma = dma
        o.slot = slot
        o.marked = False
        o.sig = None
        o.phase = self.phase
        deps = []
        for vw in r:
            for (key, lo, hi, excl) in vw.regs:
                if excl:
                    self._write(o, key, lo, hi, deps)
                else:
                    self._read(o, key, lo, hi, deps)
        for vw in w:
            for (key, lo, hi, excl) in vw.regs:
                self._write(o, key, lo, hi, deps)
        best = {}
        out = []
        for d in deps:
            if d is o:
                continue
            if d.is_dma:
                if d not in out:
                    out.append(d)
            else:
                if d.eng == "pe" and eng == "pe" and not dma:
                    continue
                b = best.get(d.eng)
                if b is None or d.idx > b.idx:
                    best[d.eng] = d
        out.extend(best.values())
        for d in out:
            d.marked = True
        o.deps = out
        lst = self.ops[eng]
        o.idx = len(lst)
        lst.append(o)
        self.allops.append(o)
        return o

    def emit(self, ctx, final_waits=()):
        nc = self.nc
        sems = {}
        counts = {}
        CH = 4000

        def getsem(name):
            if name not in sems:
                sems[name] = ctx.enter_context(nc.semaphore(name))
                counts[name] = 0
            return sems[name]

        for eng in COMPUTE:
            n = 0
            for o in self.ops[eng]:
                if o.is_dma:
                    continue
                if o.marked:
                    sname = f"s_{eng}_{n // CH}"
                    s = getsem(sname)
                    counts[sname] += 1
                    o.sig = (s, counts[sname], sname)
                    n += 1
        for o in self.allops:
            if o.is_dma:
                sname = f"d_{o.slot}"
                s = getsem(sname)
                counts[sname] += 16
                o.sig = (s, counts[sname], sname)
        self.nsems = len(sems)

        def run(eng, e):
            waited = {}
            for o in self.ops[eng]:
                for d in o.deps:
                    s, val, sname = d.sig
                    if waited.get(sname, 0) >= val:
                        continue
                    e.wait_ge(s, val)
                    waited[sname] = val
                ins = o.fn(e)
                if o.is_dma:
                    ins.then_inc(o.sig[0], 16)
                elif o.marked:
                    ins.then_inc(o.sig[0], 1)
            if eng == "sp":
                for o in final_waits:
                    s, val, sname = o.sig
                    e.wait_ge(s, val)

        with nc.Block() as block:
            @block.sync
            def _(e):
                run("sp", e)

            @block.gpsimd
            def _(e):
                run("pool", e)

            @block.scalar
            def _(e):
                run("act", e)

            @block.vector
            def _(e):
                run("dve", e)

            @block.tensor
            def _(e):
                run("pe", e)


def build(n_layers=DEPTH, dbg=None):
    nc = bass.Bass("TRN2", target_bir_lowering=False)
    dr = lambda name, shape: nc.dram_tensor(name, shape, F32, kind="ExternalInput").ap()
    x_d = dr("x", [S, D])
    pk_d = dr("pk", [128, NPK])
    cst_d = dr("cst", [128, NCST])
    cstb_d = dr("cstb", [128, NCSTB])
    fg_d = dr("fg", [128, D])
    w_ada_d = dr("w_ada", [DEPTH, D, 6 * D])
    w_in_d = dr("w_in", [DEPTH, D, INW])
    w_ao_d = dr("w_attn_o", [DEPTH, 512, D])
    w_co_d = dr("w_conv_o", [DEPTH, 512, D])
    w_out_d = dr("w_out", [DEPTH, D, D])
    w_fi_d = dr("w_ffn_in", [DEPTH, D, 2 * DFF])
    w_fo_d = dr("w_ffn_out", [DEPTH, DFF, D])
    out_d = nc.dram_tensor("out", [S, D], F32, kind="ExternalOutput").ap()
    dbg_d = {}
    if dbg:
        for name, shape in dbg.items():
            dbg_d[name] = nc.dram_tensor("dbg_" + name, shape, F32, kind="ExternalOutput").ap()

    ARENA_B = 68 * 1024
    RING_COLS = 10 * 1024
    with contextlib.ExitStack() as ctx:
        sb = lambda name, cols, dt: ctx.enter_context(nc.sbuf_tensor("sb_" + name, [128, cols], dt))
        xT_t = sb("xT", KC * S, F32)
        hT_t = sb("hT", KC * S, BF16)
        cst_t = sb("cstf", NCST, F32)
        cstb_t = sb("cstb", NCSTB, BF16)
        pk_t = sb("pk", NPK, F32)
        sm_t = sb("small", 256, F32)
        smb_t = sb("smallb", 16, BF16)
        ring_t = sb("ring", RING_COLS, BF16)
        arena_t = sb("arena", ARENA_B // 2, BF16)
        ps_t = ctx.enter_context(nc.psum_tensor("ps", [128, 4096], F32))

        P = Prog(nc)
        xT = Buf("xT", xT_t[:, :], 4)
        hT = Buf("hT", hT_t[:, :], 2)
        cst = Buf("cst", cst_t[:, :], 4)
        cstb = Buf("cstb", cstb_t[:, :], 2)
        pk = Buf("pk", pk_t[:, :], 4)
        sm = Buf("sm", sm_t[:, :], 4)
        smb = Buf("smb", smb_t[:, :], 2)
        ring = Buf("ring", ring_t[:, :], 2)
        ps = Buf("psum", ps_t[:, :], 4, 0, excl=True)

        def carve(off, ncols, dt):
            es = 4 if dt == F32 else 2
            assert off % 4 == 0 and off + ncols * es <= ARENA_B, (off, ncols, es)
            ap = arena_t[:, off // 2: off // 2 + ncols * es // 2]
            if dt == F32:
                ap = ap.bitcast(F32)
            return Buf("arena", ap, es, off)

        K1 = 1024
        cact = smb.v(0, 8)

        def mm(out, lhsT, rhs, start, stop):
            P.op("pe", lambda e: e.matmul(out.ap, lhsT=lhsT.ap, rhs=rhs.ap, start=start, stop=stop),
                 r=[lhsT, rhs], w=[out])

        def tr(out, in_, ident):
            P.op("pe", lambda e: e.transpose(out.ap, in_.ap, ident.ap), r=[in_, ident], w=[out])

        def act(out, in_, func, bias=None, scale=None, accum=None):
            r = [in_]
            kw = {}
            if bias is not None:
                kw["bias"] = bias.ap
                r.append(bias)
            if scale is not None:
                if isinstance(scale, View):
                    kw["scale"] = scale.ap
                    r.append(scale)
                else:
                    kw["scale"] = scale
            w = [out]
            if accum is not None:
                kw["accum_out"] = accum.ap
                w.append(accum)
            P.op("act", lambda e: e.activation(out=out.ap, in_=in_.ap, func=func, **kw), r=r, w=w)

        def tt(out, a, b, op, eng="dve"):
            P.op(eng, lambda e: e.tensor_tensor(out=out.ap, in0=a.ap, in1=b.ap, op=op), r=[a, b], w=[out])

        def stt(out, a, scalar, b, op0, op1):
            r = [a, b]
            if isinstance(scalar, View):
                r.append(scalar)
                sc = scalar.ap
            else:
                sc = scalar
            P.op("dve", lambda e: e.scalar_tensor_tensor(out=out.ap, in0=a.ap, scalar=sc, in1=b.ap, op0=op0, op1=op1),
                 r=r, w=[out])

        def tsm(out, a, scalar):
            r = [a]
            if isinstance(scalar, View):
                r.append(scalar)
                sc = scalar.ap
            else:
                sc = scalar
            P.op("dve", lambda e: e.tensor_scalar(out=out.ap, in0=a.ap, scalar1=sc, scalar2=None, op0=ALU.mult),
                 r=r, w=[out])

        def recip(out, a):
            P.op("dve", lambda e: e.reciprocal(out=out.ap, in_=a.ap), r=[a], w=[out])

        def cp(out, a, eng="dve"):
            P.op(eng, lambda e: e.tensor_copy(out=out.ap, in_=a.ap), r=[a], w=[out])

        def memset(out, val, eng="dve"):
            P.op(eng, lambda e: e.memset(out.ap, val), w=[out])

        def dma(eng, out, in_ap, slot, r=()):
            return P.op(eng, lambda e: e.dma_start(out=out.ap, in_=in_ap), r=list(r), w=[out], dma=True, slot=slot)

        def dma_out(eng, out_ap, in_, slot):
            return P.op(eng, lambda e: e.dma_start(out=out_ap, in_=in_.ap), r=[in_], dma=True, slot=slot)

        rstate = {"pos": 0, "n": 0, "lo": 0, "hi": RING_COLS}

        def wslab(src_ap, kcn, ncols):
            tot = kcn * ncols
            if rstate["pos"] + tot > rstate["hi"] or rstate["pos"] < rstate["lo"]:
                rstate["pos"] = rstate["lo"]
            c0 = rstate["pos"]
            rstate["pos"] += tot
            i = rstate["n"]
            rstate["n"] += 1
            dst = ring.v3(c0, c0 + tot, kcn)
            dma("pool", dst, src_ap.rearrange("(kc p) n -> p kc n", p=128), f"w{i % 16}")
            return lambda kc, a, b: ring.v(c0 + kc * ncols + a, c0 + kc * ncols + b)

        bank = lambda i, a=0, b=512: ps.v(i * 512 + a, i * 512 + b)
        pair = lambda i, a=0, b=1024: ps.v(i * 1024 + a, i * 1024 + b)
        grp = lambda i, a=0, b=2048: ps.v(i * 2048 + a, i * 2048 + b)

        SCR = 32 * 1024

        dma("sp", pk.v(), pk_d, "pk")
        dma("sp", cst.v(), cst_d, "cst")
        dma("pool", cstb.v(), cstb_d, "cstb")
        ident = cst.v(C_ID, C_ID + 128)
        rotT = cst.v(C_ROT, C_ROT + 128)
        epsc = cst.v(C_EPS, C_EPS + 1)
        onesd = cstb.v(B_ONESD, B_ONESD + 128)
        blk64 = cstb.v(B_BLK, B_BLK + 128)
        onesc = cstb.v(B_ONESC, B_ONESC + 128)
        identb = cstb.v(B_ID, B_ID + 128)
        swapb = cstb.v(B_SWAP, B_SWAP + 128)
        nidentb = cstb.v(B_NID, B_NID + 128)

        P.phase = 'xload'
        for t in range(16):
            stg = carve(SCR + (t % 4) * 4096, 1024, F32)
            dma("sp", stg.v(), x_d[t * 128:(t + 1) * 128, :], f"xs{t % 4}")
            pr = t % 4
            for kc in range(KC):
                tr(pair(pr, kc * 128, kc * 128 + 128), stg.v(kc * 128, kc * 128 + 128), ident)
            src = pair(pr)
            src.ap = src.ap.rearrange("p (a b) -> p a b", a=KC)
            dstv = xT.vs(KC, t * 128, (t + 1) * 128)
            if t % 2 == 0:
                P.op("act", lambda e, o=dstv, i=src: e.activation(out=o.ap, in_=i.ap, func=AF.Copy), r=[src], w=[dstv])
            else:
                cp(dstv, src)

        act(cact, pk.v(PK_C, PK_C + 8), AF.Silu)

        def norm_bufs(sq_off, rstd_off, tmp_offs, sg=1):
            return dict(sqs=[carve(sq_off + i * 4096, S, BF16) for i in range(2)],
                        rstd=carve(rstd_off, S, F32), tmps=[carve(o, S, F32) for o in tmp_offs], sg=sg)

        def norm_chunk(nb, kc):
            sq = nb["sqs"][kc % 2]
            act(sq.v(), xT.v(kc * S, (kc + 1) * S), AF.Square)
            for tb in range(4):
                mm(bank(nb["sg"] * 4 + tb), onesd, sq.v(tb * 512, tb * 512 + 512), kc == 0, kc == KC - 1)

        def norm_finish(nb, gs_col, shift_col):
            rstd = nb["rstd"]
            act(rstd.v(), grp(nb["sg"]), AF.Ln, bias=epsc)
            act(rstd.v(), rstd.v(), AF.Exp, scale=-0.5)
            for kc in range(KC):
                tmp = nb["tmps"][kc % 2]
                tt(tmp.v(), xT.v(kc * S, (kc + 1) * S), rstd.v(), ALU.mult)
                act(hT.v(kc * S, (kc + 1) * S), tmp.v(), AF.Identity,
                    bias=sm.v(shift_col + kc, shift_col + kc + 1), scale=sm.v(gs_col + kc, gs_col + kc + 1))

        NB2 = lambda: norm_bufs(SCR, SCR + 8192, [SCR + 16384, SCR + 24576], sg=0)
        NB1 = lambda: norm_bufs(49152, 57344, [0, 8192])

        def proj(out_fn, slab, ncol0, kcn, rhs_fn, tbs=range(4), t0=0):
            for kc in range(kcn):
                w = slab(kc, ncol0, ncol0 + 128)
                for tb in tbs:
                    mm(out_fn(tb), w, rhs_fn(kc, tb), kc == 0, kc == kcn - 1)

        hT_rhs = lambda kc, tb: hT.v(kc * S + tb * 512, kc * S + tb * 512 + 512)

        def mod_slab_mm(l_, s_, cb=0):
            slab = wslab(w_ada_d[l_, :, s_ * 256:(s_ + 1) * 256], KC, 256)

            def half(h):
                j = s_ * 2 + h
                for kc in range(KC):
                    mm(bank(6, cb + j, cb + j + 1), slab(kc, h * 128, h * 128 + 128), smb.v(kc, kc + 1), kc == 0, kc == KC - 1)
            return [lambda: half(0), lambda: half(1)]

        def mod_finish(l_, mb_, c0=0, c1=48, cb=0):
            o_ = PK_L0 + l_ * LP
            tt(sm.v(mb_ + c0, mb_ + c1), bank(6, cb + c0, cb + c1), pk.v(o_ + O_BADA + c0, o_ + O_BADA + c1), ALU.add)
            if c0 == 0:
                stt(sm.v(mb_ + 48, mb_ + 56), sm.v(mb_ + 8, mb_ + 16), 1.0, pk.v(o_ + O_GMIX, o_ + O_GMIX + 8), ALU.add, ALU.mult)
                tsm(sm.v(mb_ + 64, mb_ + 65), pk.v(o_ + O_QG, o_ + O_QG + 1), 0.125)
            if c1 == 48:
                stt(sm.v(mb_ + 56, mb_ + 64), sm.v(mb_ + 32, mb_ + 40), 1.0, pk.v(o_ + O_GFFN, o_ + O_GFFN + 8), ALU.add, ALU.mult)

        for l in range(n_layers):
            pkl = PK_L0 + l * LP
            pcol = lambda o, n=1: pk.v(pkl + o, pkl + o + n)

            mb = (l % 2) * 80
            modT = lambda a, b=None, mb=mb: sm.v(mb + a, mb + (a + 1 if b is None else b))
            if l == 0:
                P.phase = 'L0.mod'
                for s_ in range(8):
                    for f_ in mod_slab_mm(0, s_):
                        f_()
                mod_finish(0, mb, 0, 16)
            SH_M, GATE_M, SH_F, GATE_F = 0, 16, 24, 40

            P.phase = f'L{l}.norm1'
            if l == 0:
                nb1 = NB1()
                for kc in range(KC):
                    norm_chunk(nb1, kc)
            norm_finish(nb1, mb + 48, mb + SH_M)

            attnT = carve(0, 4 * S, BF16)
            ucn = carve(16384, 4 * S, BF16)
            kTa = carve(16384, S, BF16)
            kTb = carve(20480, S, BF16)
            qTs = [carve(24576 + i * 4096, S, BF16) for i in range(2)]
            Vaug = carve(SCR, 16 * 2 * 192, BF16)
            Es = [carve(SCR + 12288 + i * 2048, K1, BF16) for i in range(3)]
            sqb = carve(SCR + 18432, K1, BF16)
            rstd_q = carve(SCR + 20480, K1, F32)
            qn = carve(SCR + 24576, K1, F32)
            t1 = carve(SCR + 28672, K1, F32)
            rden = carve(SCR + 32768, K1, F32)

            sqbs = [sqb, carve(SCR + 12288, K1, BF16)]
            rstds = [rstd_q, carve(SCR + 14336, K1, F32)]
            qns = [qn, rden]
            qbuf = [carve((c + 1) * 4096, S, BF16) for c in range(3)] + [qTs[0]]

            def qk_stages(job, slab, ncol0, gcol, dst, half):
                b_ = job % 2
                zp = pair(b_)
                sq_, rs_, qn_ = sqbs[b_], rstds[b_], qns[b_]
                c0 = half * K1

                def sA():
                    proj(lambda tb: pair(b_, tb * 512, tb * 512 + 512), slab, ncol0, KC,
                         lambda kc, tb: hT_rhs(kc, half * 2 + tb), tbs=range(2))

                def sB():
                    act(sq_.v(), pair(b_), AF.Square)

                def sC():
                    for i in range(2):
                        mm(pair(2, i * 512, i * 512 + 512), blk64, sq_.v(i * 512, i * 512 + 512), True, True)

                def sD():
                    act(rs_.v(), pair(2), AF.Ln, bias=epsc)
                    act(rs_.v(), rs_.v(), AF.Exp, scale=-0.5)

                def sE():
                    stt(qn_.v(), pair(b_), gcol, rs_.v(), ALU.mult, ALU.mult)

                def sF():
                    for i in range(2):
                        mm(pair(3, i * 512, i * 512 + 512), rotT, qn_.v(i * 512, i * 512 + 512), True, True)

                def sG():
                    tt(t1.v(), qn_.v(), cst.v(C_COS + c0, C_COS + c0 + K1), ALU.mult)
                    tt(qn_.v(), pair(3), cst.v(C_SIN + c0, C_SIN + c0 + K1), ALU.mult)
                    tt(dst.v(c0, c0 + K1), t1.v(), qn_.v(), ALU.add)
                return [sA, sB, sC, sD, sE, sF, sG]

            P.phase = f'L{l}.kv'
            rstate["pos"] = 0
            slab_kv = wslab(w_in_d[l, :, 512:768], KC, 256)
            slab_q = [wslab(w_in_d[l, :, i * 256:(i + 1) * 256], KC, 256) for i in range(2)]
            rstate["lo"] = rstate["pos"]
            assert rstate["lo"] == 6144
            memset(Vaug.v(), 1.0)
            for t in range(16):
                for kc in range(KC):
                    mm(ps.v(2048 + t * 128, 2048 + t * 128 + 128), hT.v(kc * S + t * 128, kc * S + t * 128 + 128),
                       slab_kv(kc, 128, 256), kc == 0, kc == KC - 1)
            vsrc = grp(1)
            vsrc.ap = vsrc.ap.rearrange("p (t g c) -> p t g c", t=16, g=2)
            vdst = Vaug.v()
            vdst.ap = vdst.ap.rearrange("p (t g c) -> p t g c", t=16, g=2)[:, :, :, 64:128]
            cp(vdst, vsrc)
            jobs = []
            for half in range(2):
                jobs.append(qk_stages(len(jobs), slab_kv, 0, pcol(O_KG), kTa, half))
            for c in range(4):
                for half in range(2):
                    jobs.append(qk_stages(len(jobs), slab_q[c // 2], (c % 2) * 128, sm.v(mb + 64, mb + 65), qbuf[c], half))
            SKEW = 4
            for t in range(len(jobs) * SKEW + 7):
                for j, job in enumerate(jobs):
                    st_ = t - j * SKEW
                    if 0 <= st_ < 7:
                        job[st_]()
                if t == 1 * SKEW + 7:
                    pass
            Z11 = carve(SCR + 24576, S, BF16)
            Z01 = carve(SCR + 28672, S, BF16)
            for tb in range(4):
                mm(bank(tb), swapb, kTa.v(tb * 512, tb * 512 + 512), True, True)
            cp(Z11.v(0, S, 64, 128), kTa.v(0, S, 64, 128))
            memset(Z11.v(0, S, 0, 64), 0.0)
            memset(Z01.v(0, S, 0, 64), 0.0)
            act(kTb.v(0, S, 0, 64), ps.v(0, 2048, 0, 64), AF.Copy)
            act(Z01.v(0, S, 64, 128), ps.v(0, 2048, 64, 128), AF.Copy)
            memset(kTa.v(0, S, 64, 128), 0.0)
            memset(kTb.v(0, S, 64, 128), 0.0)
            kz = [[kTa, Z01], [kTb, Z11]]

            P.phase = f'L{l}.attn'
            accsb = rstd_q
            mods_pending = [(l + 1, s_, 0) for s_ in range(24)] if l + 1 < n_layers else []
            if l == 0:
                mods_pending = [(0, s_, 64) for s_ in range(8, 24)] + mods_pending
            mod_every = max(2, 256 // (len(mods_pending) + 1))
            mod_halves = []
            gstep = 0

            def s_emit(qT, g, st, idx):
                par, qh, kb = st
                kX = kz[g][par]
                for i in range(2):
                    mm(pair(idx % 2, i * 512, i * 512 + 512),
                       kX.v(kb * 128, kb * 128 + 128),
                       qT.v(qh * K1 + i * 512, qh * K1 + i * 512 + 512), True, True)

            steps = [(c, par, qh, kb) for c in range(4) for par in range(2) for qh in range(2) for kb in range(16)]
            NS = len(steps)
            s_emit(qbuf[steps[0][0]], steps[0][0] // 2, steps[0][1:], 0)
            s_emit(qbuf[steps[1][0]], steps[1][0] // 2, steps[1][1:], 1)
            for si, (c, par, qh, kb) in enumerate(steps):
                g = c // 2
                E = Es[si % 3]
                act(E.v(), pair(si % 2), AF.Exp)
                if si + 2 < NS:
                    c2 = steps[si + 2][0]
                    s_emit(qbuf[c2], c2 // 2, steps[si + 2][1:], si + 2)
                vlo = 0 if par == 1 else 64
                vb = (kb * 2 + g) * 192 + vlo
                for i in range(2):
                    mm(pair(2, i * 512, i * 512 + 512), Vaug.v(vb, vb + 128),
                       E.v(i * 512, i * 512 + 512), kb == 0, kb == 15)
                if si < 8:
                    mm(bank(7), identb, hT.v(0, 512), True, True)
                if kb == 15:
                    orow0, drow0 = (0, 64) if par == 0 else (64, 0)
                    cp(accsb.v(0, 512), bank(4))
                    cp(accsb.v(512, K1), bank(5))
                    rd = rden.v(0, K1, orow0, orow0 + 64)
                    recip(rd, accsb.v(0, K1, drow0, drow0 + 64))
                    tt(attnT.v(c * S + qh * K1, c * S + qh * K1 + K1, orow0, orow0 + 64),
                       accsb.v(0, K1, orow0, orow0 + 64), rd, ALU.mult)
                gstep += 1
                if mod_halves:
                    mod_halves.pop(0)()
                elif mods_pending and gstep % mod_every == mod_every // 2:
                    mod_halves = mod_slab_mm(*mods_pending.pop(0))
                    mod_halves.pop(0)()
            while mods_pending or mod_halves:
                if not mod_halves:
                    mod_halves = mod_slab_mm(*mods_pending.pop(0))
                mod_halves.pop(0)()
            if l == 0:
                mod_finish(0, mb, 16, 48, 64)
            if l + 1 < n_layers:
                mod_finish(l + 1, ((l + 1) % 2) * 80)
            rstate["lo"] = 0

            P.phase = f'L{l}.conv'
            sbuf_ = carve(SCR, S, BF16)
            UW = S + 32
            us = [carve(SCR + 4096 + i * (UW * 2), UW, BF16) for i in range(2)]
            diags = [carve(SCR + 4096 + 2 * UW * 2 + i * 7936, 31 * 128, BF16) for i in range(2)]
            TD = 8
            cacc = carve(SCR + 4096 + 2 * UW * 2 + 2 * 7936, S, F32)
            slab_a = [None, None]
            slab_b = [None, None]
            for c in range(4):
                if c % 2 == 0:
                    slab_a[0] = wslab(w_in_d[l, :, 768 + (c // 2) * 256: 768 + (c // 2) * 256 + 256], KC, 256)
                    slab_b[0] = wslab(w_in_d[l, :, 1280 + (c // 2) * 256: 1280 + (c // 2) * 256 + 256], KC, 256)
                u = us[c % 2]
                dg = diags[c % 2]
                for j in range(TD, 31):
                    tsm(dg.v(j * 128, j * 128 + 128), identb, pcol(O_DW + c * 31 + j))
                memset(u.v(0, 16), 0.0)
                memset(u.v(16 + S, UW), 0.0)
                proj(lambda tb: bank(4 + tb), slab_b[0], (c % 2) * 128, KC, hT_rhs)
                for th in range(2):
                    act(sbuf_.v(th * K1, th * K1 + K1), pair(2 + th), AF.Sigmoid)
                proj(lambda tb: bank(tb), slab_a[0], (c % 2) * 128, KC, hT_rhs)
                for th in range(2):
                    tt(u.v(16 + th * K1, 16 + th * K1 + K1), pair(th), sbuf_.v(th * K1, th * K1 + K1), ALU.mult)
                for j in range(TD):
                    src = u.v(j + 1, j + 1 + S)
                    wj = pcol(O_DW + c * 31 + j)
                    if j == 0:
                        tsm(cacc.v(), src, wj)
                    else:
                        stt(cacc.v(), src, wj, cacc.v(), ALU.mult, ALU.add)
                for tb in range(4):
                    for j in range(TD, 31):
                        mm(bank(4 + tb), dg.v(j * 128, j * 128 + 128), u.v(tb * 512 + j + 1, tb * 512 + j + 1 + 512), j == TD, j == 30)
                stt(ucn.v(c * S, (c + 1) * S), grp(1), pcol(O_DWB + c), cacc.v(), ALU.add, ALU.add)
            P.phase = f'L{l}.ln'
            sqs = [carve(SCR + i * 4096, S, BF16) for i in range(2)]
            m2 = carve(SCR + 8192, S, F32)
            tmps = [carve(SCR + 16384 + i * 8192, S, F32) for i in range(2)]
            for c in range(4):
                sq = sqs[c % 2]
                act(sq.v(), ucn.v(c * S, (c + 1) * S), AF.Square)
                for tb in range(4):
                    mm(bank(tb), onesc, ucn.v(c * S + tb * 512, c * S + tb * 512 + 512), c == 0, c == 3)
                for tb in range(4):
                    mm(bank(4 + tb), onesc, sq.v(tb * 512, tb * 512 + 512), c == 0, c == 3)
            act(m2.v(), grp(0), AF.Square)
            meanb = sqs[0]
            act(meanb.v(), grp(0), AF.Copy)
            tt(m2.v(), grp(1), m2.v(), ALU.subtract)
            act(m2.v(), m2.v(), AF.Ln, bias=epsc)
            act(m2.v(), m2.v(), AF.Exp, scale=-0.5)
            for c in range(4):
                gi_ = (c + 1) % 2
                for tb in range(4):
                    mm(bank(gi_ * 4 + tb), identb, ucn.v(c * S + tb * 512, c * S + tb * 512 + 512), True, False)
                    mm(bank(gi_ * 4 + tb), nidentb, meanb.v(tb * 512, tb * 512 + 512), False, True)
                tmp = tmps[c % 2]
                tt(tmp.v(), grp(gi_), m2.v(), ALU.mult)
                act(ucn.v(c * S, (c + 1) * S), tmp.v(), AF.Silu, bias=pcol(O_LNB + c), scale=pcol(O_LNG + c))

            P.phase = f'L{l}.merge'
            sgc = carve(SCR, S, BF16)
            sga = carve(SCR + 4096, S, BF16)
            mbuf = carve(SCR + 8192, S, BF16)
            tbuf = carve(SCR + 12288, S, F32)
            merged = carve(SCR + 20480, 4 * S, BF16)
            gcount = 0
            for hj in range(2):
                for jj in range(4):
                    j = hj * 4 + jj
                    if j % 2 == 0:
                        s_gc = wslab(w_in_d[l, :, 1792 + (j // 2) * 256: 1792 + (j // 2) * 256 + 256], KC, 256)
                        s_co = wslab(w_co_d[l, :, (j // 2) * 256:(j // 2) * 256 + 256], 4, 256)
                        s_ga = wslab(w_in_d[l, :, 2816 + (j // 2) * 256: 2816 + (j // 2) * 256 + 256], KC, 256)
                        s_ao = wslab(w_ao_d[l, :, (j // 2) * 256:(j // 2) * 256 + 256], 4, 256)
                    n0 = (j % 2) * 128
                    proj(lambda tb: bank(4 + tb), s_gc, n0, KC, hT_rhs)
                    act(sgc.v(), grp(1), AF.Sigmoid)
                    proj(lambda tb: bank(tb), s_co, n0, 4,
                         lambda kc, tb: ucn.v(kc * S + tb * 512, kc * S + tb * 512 + 512))
                    stt(mbuf.v(), grp(0), pcol(O_BCO + j), sgc.v(), ALU.add, ALU.mult)
                    proj(lambda tb: bank(4 + tb), s_ga, n0, KC, hT_rhs)
                    act(sga.v(), grp(1), AF.Sigmoid)
                    proj(lambda tb: bank(tb), s_ao, n0, 4,
                         lambda kc, tb: attnT.v(kc * S + tb * 512, kc * S + tb * 512 + 512))
                    tt(tbuf.v(), grp(0), sga.v(), ALU.mult)
                    tt(merged.v(jj * S, (jj + 1) * S), tbuf.v(), mbuf.v(), ALU.add)
                if hj == 1:
                    nb2 = NB2()
                for n in range(8):
                    if n % 2 == 0:
                        s_wo = wslab(w_out_d[l, hj * 512:(hj + 1) * 512, (n // 2) * 256:(n // 2) * 256 + 256], 4, 256)
                    if hj == 0:
                        gi = (n + 1) % 2
                        proj(lambda tb: bank(gi * 4 + tb), s_wo, (n % 2) * 128, 4,
                             lambda kc, tb: merged.v(kc * S + tb * 512, kc * S + tb * 512 + 512))
                        stt(xT.v(n * S, (n + 1) * S), grp(gi), modT(GATE_M + n), xT.v(n * S, (n + 1) * S), ALU.mult, ALU.add)
                    else:
                        for th in range(2):
                            proj(lambda tb: pair(2 + th, tb * 512, tb * 512 + 512), s_wo, (n % 2) * 128, 4,
                                 lambda kc, tb: merged.v(kc * S + (th * 2 + tb) * 512, kc * S + (th * 2 + tb) * 512 + 512),
                                 tbs=range(2))
                            xs_ = xT.v(n * S + th * K1, n * S + th * K1 + K1)
                            stt(xs_, pair(2 + th), modT(GATE_M + n), xs_, ALU.mult, ALU.add)
                        if n >= 1:
                            norm_chunk(nb2, n - 1)
                if hj == 1:
                    norm_chunk(nb2, 7)

            P.phase = f'L{l}.norm2'
            norm_finish(nb2, mb + 56, mb + SH_F)
            P.phase = f'L{l}.ffn'
            actb = carve(0, 12 * S, BF16)
            sgs = [carve(12 * S * 2 + i * 4096, S, BF16) for i in range(2)]
            f0 = 0
            for hf, nf in enumerate((12, 10)):
                for ff in range(nf):
                    f = f0 + ff
                    if f % 2 == 0:
                        s_g = wslab(w_fi_d[l, :, (f // 2) * 256:(f // 2) * 256 + 256], KC, 256)
                        s_u = wslab(w_fi_d[l, :, DFF + (f // 2) * 256: DFF + (f // 2) * 256 + 256], KC, 256)
                    sg = sgs[f % 2]
                    proj(lambda tb: bank(tb), s_g, (f % 2) * 128, KC, hT_rhs)
                    act(sg.v(), grp(0), AF.Silu)
                    proj(lambda tb: bank(4 + tb), s_u, (f % 2) * 128, KC, hT_rhs)
                    tt(actb.v(ff * S, (ff + 1) * S), grp(1), sg.v(), ALU.mult)
                ovl = (hf == 1 and l + 1 < n_layers)
                if ovl:
                    nb1 = NB1()
                for n in range(8):
                    s_fo = wslab(w_fo_d[l, f0 * 128:(f0 + nf) * 128, n * 128:(n + 1) * 128], nf, 128)
                    if not ovl:
                        gi = n % 2
                        proj(lambda tb: bank(gi * 4 + tb), s_fo, 0, nf,
                             lambda kc, tb: actb.v(kc * S + tb * 512, kc * S + tb * 512 + 512))
                        stt(xT.v(n * S, (n + 1) * S), grp(gi), modT(GATE_F + n), xT.v(n * S, (n + 1) * S), ALU.mult, ALU.add)
                    else:
                        for th in range(2):
                            proj(lambda tb: pair(th, tb * 512, tb * 512 + 512), s_fo, 0, nf,
                                 lambda kc, tb: actb.v(kc * S + (th * 2 + tb) * 512, kc * S + (th * 2 + tb) * 512 + 512),
                                 tbs=range(2))
                            xs_ = xT.v(n * S + th * K1, n * S + th * K1 + K1)
                            stt(xs_, pair(th), modT(GATE_F + n), xs_, ALU.mult, ALU.add)
                        if n >= 1:
                            norm_chunk(nb1, n - 1)
                if ovl:
                    norm_chunk(nb1, 7)
                f0 += nf

        P.phase = 'final'
        fg = carve(0, D, F32)
        dma("sp", fg.v(), fg_d, "fg")
        junk = carve(4096, D, F32)
        ostg = [carve(8192 + i * 4096, D, F32) for i in range(2)]
        stat = carve(16384, 64, F32)
        memset(stat.v(), 0.0)
        outs = []
        for t in range(16):
            pr = t % 4
            for kc in range(KC):
                tr(pair(pr, kc * 128, kc * 128 + 128), xT.v(kc * S + t * 128, kc * S + t * 128 + 128), ident)
            ss = stat.v(2 * t, 2 * t + 1)
            sd = stat.v(2 * t + 1, 2 * t + 2)
            act(junk.v(), pair(pr), AF.Square, accum=ss)
            act(sd, ss, AF.Sqrt, bias=epsc, scale=1.0 / D)
            recip(sd, sd)
            o = ostg[t % 2]
            stt(o.v(), pair(pr), sd, fg.v(), ALU.mult, ALU.mult)
            outs.append(dma_out("sp", out_d[t * 128:(t + 1) * 128, :], o.v(), f"o{t % 2}"))
        if dbg and "sm" in dbg_d:
            outs.append(dma_out("sp", dbg_d["sm"], sm.v(), "dbgsm"))
            outs = outs[-3:]
        else:
            outs = outs[-2:]
        P.emit(ctx, final_waits=outs)
        build.stats = {k: len(v) for k, v in P.ops.items()}
        build.stats["sems"] = P.nsems
        build.prog = P
    return nc


def _consts():
    cst = np.zeros((128, NCST), np.float32)
    cst[:, C_ID:C_ID + 128] = np.eye(128, dtype=np.float32)
    R = np.zeros((128, 128), np.float32)
    for p in range(128):
        i = p % 32
        if i < 16:
            R[p, p + 16] = -1.0
        else:
            R[p, p - 16] = 1.0
    cst[:, C_ROT:C_ROT + 128] = R.T
    tpos = np.arange(S)
    row = (tpos // 64).astype(np.float32)
    col = (tpos % 64).astype(np.float32)
    inv = (np.float32(10000.0) ** (-np.arange(0, 32, 2, dtype=np.float32) / np.float32(32))).astype(np.float32)
    for p in range(128):
        i = p % 16
        seg = (p % 64) // 32
        pos = row if seg == 0 else col
        ang = (pos * inv[i]).astype(np.float32)
        cst[p, C_COS:C_COS + S] = np.cos(ang)
        cst[p, C_SIN:C_SIN + S] = np.sin(ang)
    cst[:, C_EPS] = EPS
    cst[:, C_ONE] = 1.0
    cb = np.zeros((128, NCSTB), np.float32)
    cb[:, B_ONESD:B_ONESD + 128] = 1.0 / D
    blk = np.zeros((128, 128), np.float32)
    blk[:64, :64] = 1.0 / 64
    blk[64:, 64:] = 1.0 / 64
    cb[:, B_BLK:B_BLK + 128] = blk
    cb[:, B_ONESC:B_ONESC + 128] = 1.0 / 512
    cb[:, B_ID:B_ID + 128] = np.eye(128, dtype=np.float32)
    sw = np.zeros((128, 128), np.float32)
    for p in range(128):
        sw[(p + 64) % 128, p] = 1.0
    cb[:, B_SWAP:B_SWAP + 128] = sw
    cb[:, B_NID:B_NID + 128] = -np.eye(128, dtype=np.float32)
    return cst, cb


def _pack(b, c, b_ada, norm_mix_g, q_norm_g, k_norm_g, conv_dw, conv_dw_b, conv_ln_g, conv_ln_b,
          b_conv_o, norm_ffn_g):
    pk = np.zeros((128, NPK), np.float32)
    col = lambda v: np.ascontiguousarray(v.reshape(-1, 128).T)
    pk[:, PK_C:PK_C + 8] = col(c[b])
    for l in range(DEPTH):
        o = PK_L0 + l * LP
        pk[:, o + O_GMIX:o + O_GMIX + 8] = col(norm_mix_g[l])
        pk[:, o + O_GFFN:o + O_GFFN + 8] = col(norm_ffn_g[l])
        pk[:, o + O_QG] = np.tile(q_norm_g[l], 2)
        pk[:, o + O_KG] = np.tile(k_norm_g[l], 2)
        dw = conv_dw[l].reshape(31, 4, 128).transpose(2, 1, 0).reshape(128, 124)
        pk[:, o + O_DW:o + O_DW + 124] = dw
        pk[:, o + O_DWB:o + O_DWB + 4] = col(conv_dw_b[l])
        pk[:, o + O_LNG:o + O_LNG + 4] = col(conv_ln_g[l])
        pk[:, o + O_LNB:o + O_LNB + 4] = col(conv_ln_b[l])
        pk[:, o + O_BCO:o + O_BCO + 8] = col(b_conv_o[l])
        pk[:, o + O_BADA:o + O_BADA + 48] = col(b_ada[l])
    return pk


_NC_CACHE = {}


def make_in_maps(cores, x, c, w_ada, b_ada, norm_mix_g, w_in, q_norm_g, k_norm_g, w_attn_o,
                 conv_dw, conv_dw_b, conv_ln_g, conv_ln_b, w_conv_o, b_conv_o, w_out,
                 norm_ffn_g, w_ffn_in, w_ffn_out, final_norm_g):
    f = lambda a: np.ascontiguousarray(np.asarray(a, dtype=np.float32))
    cst, cb = _consts()
    fg = np.ascontiguousarray(np.broadcast_to(f(final_norm_g)[None, :], (128, D)))
    shared = {"cst": cst, "cstb": cb, "fg": fg, "w_ada": f(w_ada), "w_in": f(w_in), "w_attn_o": f(w_attn_o),
              "w_conv_o": f(w_conv_o), "w_out": f(w_out), "w_ffn_in": f(w_ffn_in), "w_ffn_out": f(w_ffn_out)}
    x = f(x)
    maps = []
    for b in cores:
        m = dict(shared)
        m["x"] = np.ascontiguousarray(x[b])
        m["pk"] = _pack(b, f(c), f(b_ada), f(norm_mix_g), f(q_norm_g), f(k_norm_g), f(conv_dw), f(conv_dw_b),
                        f(conv_ln_g), f(conv_ln_b), f(b_conv_o), f(norm_ffn_g))
        maps.append(m)
    return maps


def kernel(**inputs):
    if "nc" not in _NC_CACHE:
        _NC_CACHE["nc"] = build(DEPTH)
    nc = _NC_CACHE["nc"]
    maps = make_in_maps(list(range(NCORES)), **inputs)
    res = run_bass_kernel_spmd(nc, maps, core_ids=list(range(NCORES)))
    out = np.stack([np.asarray(r["out"], dtype=np.float32) for r in res.results], axis=0)
    return out
```

```python
import contextlib
import numpy as np
import concourse.bass as bass
import concourse.mybir as mybir
from concourse.bass_utils import run_bass_kernel_spmd

F32 = mybir.dt.float32
BF16 = mybir.dt.bfloat16
F32R = mybir.dt.float32r
AF = mybir.ActivationFunctionType
ALU = mybir.AluOpType

S = 2048
D = 1024
KC = 8
DEPTH = 4
DFF = 2816
INW = 3840
EPS = 1e-6
NCORES = 8

COMPUTE = ("pe", "act", "dve", "pool")

LP = 210
PK_C = 0
PK_L0 = 8
O_GMIX, O_GFFN, O_QG, O_KG, O_DW, O_DWB, O_LNG, O_LNB, O_BCO, O_BADA = 0, 8, 16, 17, 18, 142, 146, 150, 154, 162
NPK = PK_L0 + DEPTH * LP
C_ID, C_ROT, C_COS, C_SIN, C_EPS, C_ONE = 0, 128, 256, 2304, 4352, 4353
NCST = 4354
B_ONESD, B_BLK, B_ONESC, B_ID, B_SWAP, B_NID = 0, 128, 256, 384, 512, 640
NCSTB = 768


class View:
    __slots__ = ("ap", "regs")

    def __init__(self, ap, regs):
        self.ap = ap
        self.regs = regs


class Buf:
    def __init__(self, key, ap2d, esize, byte_off=0, excl=False):
        self.key = key
        self.ap = ap2d
        self.esize = esize
        self.off = byte_off
        self.excl = excl
        self.n = ap2d.shape[1]

    def _reg(self, c0, c1):
        lo = self.off + c0 * self.esize
        hi = self.off + c1 * self.esize
        if self.excl:
            lo = (lo // 2048) * 2048
            hi = ((hi + 2047) // 2048) * 2048
        return (self.key, lo, hi, self.excl)

    def v(self, c0=None, c1=None, p0=None, p1=None):
        c0 = 0 if c0 is None else c0
        c1 = self.n if c1 is None else c1
        assert 0 <= c0 < c1 <= self.n, (self.key, c0, c1, self.n)
        ap = self.ap[:, c0:c1] if p0 is None else self.ap[p0:p1, c0:c1]
        return View(ap, [self._reg(c0, c1)])

    def v3(self, c0, c1, a, p0=None, p1=None):
        vw = self.v(c0, c1, p0, p1)
        vw.ap = vw.ap.rearrange("p (a b) -> p a b", a=a)
        return vw

    def vs(self, a, t0, t1):
        b = self.n // a
        ap = self.ap.rearrange("p (a b) -> p a b", a=a)[:, :, t0:t1]
        return View(ap, [self._reg(i * b + t0, i * b + t1) for i in range(a)])


class Op:
    __slots__ = ("eng", "fn", "deps", "marked", "sig", "idx", "is_dma", "slot", "phase")


class Prog:
    def __init__(self, nc):
        self.nc = nc
        self.ops = {k: [] for k in ("pe", "act", "dve", "pool", "sp")}
        self.seg = {}
        self.allops = []
        self.phase = ''

    def _read(self, op, key, lo, hi, deps):
        for s in self.seg.get(key, ()):
            if s[0] < hi and lo < s[1]:
                if s[2] is not None:
                    deps.append(s[2])
                s[3][(op.eng, op.is_dma and id(op))] = op

    def _write(self, op, key, lo, hi, deps):
        segs = self.seg.get(key, [])
        new = []
        for s in segs:
            if s[0] < hi and lo < s[1]:
                if s[2] is not None:
                    deps.append(s[2])
                deps.extend(s[3].values())
                if s[0] < lo:
                    new.append([s[0], lo, s[2], dict(s[3])])
                if hi < s[1]:
                    new.append([hi, s[1], s[2], dict(s[3])])
            else:
                new.append(s)
        new.append([lo, hi, op, {}])
        self.seg[key] = new

    def op(self, eng, fn, r=(), w=(), dma=False, slot=None):
        o = Op()
        o.eng = eng
        o.fn = fn
        o.is_dma = dma
        o.slot = slot
        o.marked = False
        o.sig = None
        o.phase = self.phase
        deps = []
        for vw in r:
            for (key, lo, hi, excl) in vw.regs:
                if excl:
                    self._write(o, key, lo, hi, deps)
                else:
                    self._read(o, key, lo, hi, deps)
        for vw in w:
            for (key, lo, hi, excl) in vw.regs:
                self._write(o, key, lo, hi, deps)
        best = {}
        out = []
        for d in deps:
            if d is o:
                continue
            if d.is_dma:
                if d not in out:
                    out.append(d)
            else:
                if d.eng == "pe" and eng == "pe" and not dma:
                    continue
                b = best.get(d.eng)
                if b is None or d.idx > b.idx:
                    best[d.eng] = d
        out.extend(best.values())
        for d in out:
            d.marked = True
        o.deps = out
        lst = self.ops[eng]
        o.idx = len(lst)
        lst.append(o)
        self.allops.append(o)
        return o

    def emit(self, ctx, final_waits=()):
        nc = self.nc
        sems = {}
        counts = {}
        CH = 4000

        def getsem(name):
            if name not in sems:
                sems[name] = ctx.enter_context(nc.semaphore(name))
                counts[name] = 0
            return sems[name]

        for eng in COMPUTE:
            n = 0
            for o in self.ops[eng]:
                if o.is_dma:
                    continue
                if o.marked:
                    sname = f"s_{eng}_{n // CH}"
                    s = getsem(sname)
                    counts[sname] += 1
                    o.sig = (s, counts[sname], sname)
                    n += 1
        for o in self.allops:
            if o.is_dma:
                sname = f"d_{o.slot}"
                s = getsem(sname)
                counts[sname] += 16
                o.sig = (s, counts[sname], sname)
        self.nsems = len(sems)

        def run(eng, e):
            waited = {}
            for o in self.ops[eng]:
                for d in o.deps:
                    s, val, sname = d.sig
                    if waited.get(sname, 0) >= val:
                        continue
                    e.wait_ge(s, val)
                    waited[sname] = val
                ins = o.fn(e)
                if o.is_dma:
                    ins.then_inc(o.sig[0], 16)
                elif o.marked:
                    ins.then_inc(o.sig[0], 1)
            if eng == "sp":
                for o in final_waits:
                    s, val, sname = o.sig
                    e.wait_ge(s, val)

        with nc.Block() as block:
            @block.sync
            def _(e):
                run("sp", e)

            @block.gpsimd
            def _(e):
                run("pool", e)

            @block.scalar
            def _(e):
                run("act", e)

            @block.vector
            def _(e):
                run("dve", e)

            @block.tensor
            def _(e):
                run("pe", e)


def build(n_layers=DEPTH, dbg=None):
    nc = bass.Bass("TRN2", target_bir_lowering=False)
    dr = lambda name, shape: nc.dram_tensor(name, shape, F32, kind="ExternalInput").ap()
    x_d = dr("x", [S, D])
    pk_d = dr("pk", [128, NPK])
    cst_d = dr("cst", [128, NCST])
    cstb_d = dr("cstb", [128, NCSTB])
    fg_d = dr("fg", [128, D])
    w_ada_d = dr("w_ada", [DEPTH, D, 6 * D])
    w_in_d = dr("w_in", [DEPTH, D, INW])
    w_ao_d = dr("w_attn_o", [DEPTH, 512, D])
    w_co_d = dr("w_conv_o", [DEPTH, 512, D])
    w_out_d = dr("w_out", [DEPTH, D, D])
    w_fi_d = dr("w_ffn_in", [DEPTH, D, 2 * DFF])
    w_fo_d = dr("w_ffn_out", [DEPTH, DFF, D])
    out_d = nc.dram_tensor("out", [S, D], F32, kind="ExternalOutput").ap()
    dbg_d = {}
    if dbg:
        for name, shape in dbg.items():
            dbg_d[name] = nc.dram_tensor("dbg_" + name, shape, F32, kind="ExternalOutput").ap()

    ARENA_B = 68 * 1024
    RING_COLS = 10 * 1024
    with contextlib.ExitStack() as ctx:
        sb = lambda name, cols, dt: ctx.enter_context(nc.sbuf_tensor("sb_" + name, [128, cols], dt))
        xT_t = sb("xT", KC * S, F32)
        hT_t = sb("hT", KC * S, BF16)
        cst_t = sb("cstf", NCST, F32)
        cstb_t = sb("cstb", NCSTB, BF16)
        pk_t = sb("pk", NPK, F32)
        sm_t = sb("small", 256, F32)
        smb_t = sb("smallb", 16, BF16)
        ring_t = sb("ring", RING_COLS, BF16)
        arena_t = sb("arena", ARENA_B // 2, BF16)
        ps_t = ctx.enter_context(nc.psum_tensor("ps", [128, 4096], F32))

        P = Prog(nc)
        xT = Buf("xT", xT_t[:, :], 4)
        hT = Buf("hT", hT_t[:, :], 2)
        cst = Buf("cst", cst_t[:, :], 4)
        cstb = Buf("cstb", cstb_t[:, :], 2)
        pk = Buf("pk", pk_t[:, :], 4)
        sm = Buf("sm", sm_t[:, :], 4)
        smb = Buf("smb", smb_t[:, :], 2)
        ring = Buf("ring", ring_t[:, :], 2)
        ps = Buf("psum", ps_t[:, :], 4, 0, excl=True)

        def carve(off, ncols, dt):
            es = 4 if dt == F32 else 2
            assert off % 4 == 0 and off + ncols * es <= ARENA_B, (off, ncols, es)
            ap = arena_t[:, off // 2: off // 2 + ncols * es // 2]
            if dt == F32:
                ap = ap.bitcast(F32)
            return Buf("arena", ap, es, off)

        K1 = 1024
        cact = smb.v(0, 8)

        def mm(out, lhsT, rhs, start, stop):
            P.op("pe", lambda e: e.matmul(out.ap, lhsT=lhsT.ap, rhs=rhs.ap, start=start, stop=stop),
                 r=[lhsT, rhs], w=[out])

        def tr(out, in_, ident):
            P.op("pe", lambda e: e.transpose(out.ap, in_.ap, ident.ap), r=[in_, ident], w=[out])

        def act(out, in_, func, bias=None, scale=None, accum=None):
            r = [in_]
            kw = {}
            if bias is not None:
                kw["bias"] = bias.ap
                r.append(bias)
            if scale is not None:
                if isinstance(scale, View):
                    kw["scale"] = scale.ap
                    r.append(scale)
                else:
                    kw["scale"] = scale
            w = [out]
            if accum is not None:
                kw["accum_out"] = accum.ap
                w.append(accum)
            P.op("act", lambda e: e.activation(out=out.ap, in_=in_.ap, func=func, **kw), r=r, w=w)

        def tt(out, a, b, op, eng="dve"):
            P.op(eng, lambda e: e.tensor_tensor(out=out.ap, in0=a.ap, in1=b.ap, op=op), r=[a, b], w=[out])

        def stt(out, a, scalar, b, op0, op1):
            r = [a, b]
            if isinstance(scalar, View):
                r.append(scalar)
                sc = scalar.ap
            else:
                sc = scalar
            P.op("dve", lambda e: e.scalar_tensor_tensor(out=out.ap, in0=a.ap, scalar=sc, in1=b.ap, op0=op0, op1=op1),
                 r=r, w=[out])

        def tsm(out, a, scalar):
            r = [a]
            if isinstance(scalar, View):
                r.append(scalar)
                sc = scalar.ap
            else:
                sc = scalar
            P.op("dve", lambda e: e.tensor_scalar(out=out.ap, in0=a.ap, scalar1=sc, scalar2=None, op0=ALU.mult),
                 r=r, w=[out])

        def recip(out, a):
            P.op("dve", lambda e: e.reciprocal(out=out.ap, in_=a.ap), r=[a], w=[out])

        def cp(out, a, eng="dve"):
            P.op(eng, lambda e: e.tensor_copy(out=out.ap, in_=a.ap), r=[a], w=[out])

        def memset(out, val, eng="dve"):
            P.op(eng, lambda e: e.memset(out.ap, val), w=[out])

        def dma(eng, out, in_ap, slot, r=()):
            return P.op(eng, lambda e: e.dma_start(out=out.ap, in_=in_ap), r=list(r), w=[out], dma=True, slot=slot)

        def dma_out(eng, out_ap, in_, slot):
            return P.op(eng, lambda e: e.dma_start(out=out_ap, in_=in_.ap), r=[in_], dma=True, slot=slot)

        rstate = {"pos": 0, "n": 0, "lo": 0, "hi": RING_COLS}

        def wslab(src_ap, kcn, ncols):
            tot = kcn * ncols
            if rstate["pos"] + tot > rstate["hi"] or rstate["pos"] < rstate["lo"]:
                rstate["pos"] = rstate["lo"]
            c0 = rstate["pos"]
            rstate["pos"] += tot
            i = rstate["n"]
            rstate["n"] += 1
            dst = ring.v3(c0, c0 + tot, kcn)
            dma("pool", dst, src_ap.rearrange("(kc p) n -> p kc n", p=128), f"w{i % 16}")
            return lambda kc, a, b: ring.v(c0 + kc * ncols + a, c0 + kc * ncols + b)

        bank = lambda i, a=0, b=512: ps.v(i * 512 + a, i * 512 + b)
        pair = lambda i, a=0, b=1024: ps.v(i * 1024 + a, i * 1024 + b)
        grp = lambda i, a=0, b=2048: ps.v(i * 2048 + a, i * 2048 + b)

        SCR = 32 * 1024

        dma("sp", pk.v(), pk_d, "pk")
        dma("sp", cst.v(), cst_d, "cst")
        dma("pool", cstb.v(), cstb_d, "cstb")
        ident = cst.v(C_ID, C_ID + 128)
        rotT = cst.v(C_ROT, C_ROT + 128)
        epsc = cst.v(C_EPS, C_EPS + 1)
        onesd = cstb.v(B_ONESD, B_ONESD + 128)
        blk64 = cstb.v(B_BLK, B_BLK + 128)
        onesc = cstb.v(B_ONESC, B_ONESC + 128)
        identb = cstb.v(B_ID, B_ID + 128)
        swapb = cstb.v(B_SWAP, B_SWAP + 128)
        nidentb = cstb.v(B_NID, B_NID + 128)

        P.phase = 'xload'
        for t in range(16):
            stg = carve(SCR + (t % 4) * 4096, 1024, F32)
            dma("sp", stg.v(), x_d[t * 128:(t + 1) * 128, :], f"xs{t % 4}")
            pr = t % 4
            for kc in range(KC):
                tr(pair(pr, kc * 128, kc * 128 + 128), stg.v(kc * 128, kc * 128 + 128), ident)
            src = pair(pr)
            src.ap = src.ap.rearrange("p (a b) -> p a b", a=KC)
            dstv = xT.vs(KC, t * 128, (t + 1) * 128)
            if t % 2 == 0:
                P.op("act", lambda e, o=dstv, i=src: e.activation(out=o.ap, in_=i.ap, func=AF.Copy), r=[src], w=[dstv])
            else:
                cp(dstv, src)

        act(cact, pk.v(PK_C, PK_C + 8), AF.Silu)

        def norm_bufs(sq_off, rstd_off, tmp_offs, sg=1):
            return dict(sqs=[carve(sq_off + i * 4096, S, BF16) for i in range(2)],
                        rstd=carve(rstd_off, S, F32), tmps=[carve(o, S, F32) for o in tmp_offs], sg=sg)

        def norm_chunk(nb, kc):
            sq = nb["sqs"][kc % 2]
            act(sq.v(), xT.v(kc * S, (kc + 1) * S), AF.Square)
            for tb in range(4):
                mm(bank(nb["sg"] * 4 + tb), onesd, sq.v(tb * 512, tb * 512 + 512), kc == 0, kc == KC - 1)

        def norm_finish(nb, gs_col, shift_col):
            rstd = nb["rstd"]
            act(rstd.v(), grp(nb["sg"]), AF.Ln, bias=epsc)
            act(rstd.v(), rstd.v(), AF.Exp, scale=-0.5)
            for kc in range(KC):
                tmp = nb["tmps"][kc % 2]
                tt(tmp.v(), xT.v(kc * S, (kc + 1) * S), rstd.v(), ALU.mult)
                act(hT.v(kc * S, (kc + 1) * S), tmp.v(), AF.Identity,
                    bias=sm.v(shift_col + kc, shift_col + kc + 1), scale=sm.v(gs_col + kc, gs_col + kc + 1))

        NB2 = lambda: norm_bufs(SCR, SCR + 8192, [SCR + 16384, SCR + 24576], sg=0)
        NB1 = lambda: norm_bufs(49152, 57344, [0, 8192])

        def proj(out_fn, slab, ncol0, kcn, rhs_fn, tbs=range(4), t0=0):
            for kc in range(kcn):
                w = slab(kc, ncol0, ncol0 + 128)
                for tb in tbs:
                    mm(out_fn(tb), w, rhs_fn(kc, tb), kc == 0, kc == kcn - 1)

        hT_rhs = lambda kc, tb: hT.v(kc * S + tb * 512, kc * S + tb * 512 + 512)

        def mod_slab_mm(l_, s_, cb=0):
            slab = wslab(w_ada_d[l_, :, s_ * 256:(s_ + 1) * 256], KC, 256)

            def half(h):
                j = s_ * 2 + h
                for kc in range(KC):
                    mm(bank(6, cb + j, cb + j + 1), slab(kc, h * 128, h * 128 + 128), smb.v(kc, kc + 1), kc == 0, kc == KC - 1)
            return [lambda: half(0), lambda: half(1)]

        def mod_finish(l_, mb_, c0=0, c1=48, cb=0):
            o_ = PK_L0 + l_ * LP
            tt(sm.v(mb_ + c0, mb_ + c1), bank(6, cb + c0, cb + c1), pk.v(o_ + O_BADA + c0, o_ + O_BADA + c1), ALU.add)
            if c0 == 0:
                stt(sm.v(mb_ + 48, mb_ + 56), sm.v(mb_ + 8, mb_ + 16), 1.0, pk.v(o_ + O_GMIX, o_ + O_GMIX + 8), ALU.add, ALU.mult)
                tsm(sm.v(mb_ + 64, mb_ + 65), pk.v(o_ + O_QG, o_ + O_QG + 1), 0.125)
            if c1 == 48:
                stt(sm.v(mb_ + 56, mb_ + 64), sm.v(mb_ + 32, mb_ + 40), 1.0, pk.v(o_ + O_GFFN, o_ + O_GFFN + 8), ALU.add, ALU.mult)

        for l in range(n_layers):
            pkl = PK_L0 + l * LP
            pcol = lambda o, n=1: pk.v(pkl + o, pkl + o + n)

            mb = (l % 2) * 80
            modT = lambda a, b=None, mb=mb: sm.v(mb + a, mb + (a + 1 if b is None else b))
            if l == 0:
                P.phase = 'L0.mod'
                for s_ in range(8):
                    for f_ in mod_slab_mm(0, s_):
                        f_()
                mod_finish(0, mb, 0, 16)
            SH_M, GATE_M, SH_F, GATE_F = 0, 16, 24, 40

            P.phase = f'L{l}.norm1'
            if l == 0:
                nb1 = NB1()
                for kc in range(KC):
                    norm_chunk(nb1, kc)
            norm_finish(nb1, mb + 48, mb + SH_M)

            attnT = carve(0, 4 * S, BF16)
            ucn = carve(16384, 4 * S, BF16)
            kTa = carve(16384, S, BF16)
            kTb = carve(20480, S, BF16)
            qTs = [carve(24576 + i * 4096, S, BF16) for i in range(2)]
            Vaug = carve(SCR, 16 * 2 * 192, BF16)
            Es = [carve(SCR + 12288 + i * 2048, K1, BF16) for i in range(3)]
            sqb = carve(SCR + 18432, K1, BF16)
            rstd_q = carve(SCR + 20480, K1, F32)
            qn = carve(SCR + 24576, K1, F32)
            t1 = carve(SCR + 28672, K1, F32)
            rden = carve(SCR + 32768, K1, F32)

            sqbs = [sqb, carve(SCR + 12288, K1, BF16)]
            rstds = [rstd_q, carve(SCR + 14336, K1, F32)]
            qns = [qn, rden]
            qbuf = [carve((c + 1) * 4096, S, BF16) for c in range(3)] + [qTs[0]]

            def qk_stages(job, slab, ncol0, gcol, dst, half):
                b_ = job % 2
                zp = pair(b_)
                sq_, rs_, qn_ = sqbs[b_], rstds[b_], qns[b_]
                c0 = half * K1

                def sA():
                    proj(lambda tb: pair(b_, tb * 512, tb * 512 + 512), slab, ncol0, KC,
                         lambda kc, tb: hT_rhs(kc, half * 2 + tb), tbs=range(2))

                def sB():
                    act(sq_.v(), pair(b_), AF.Square)

                def sC():
                    for i in range(2):
                        mm(pair(2, i * 512, i * 512 + 512), blk64, sq_.v(i * 512, i * 512 + 512), True, True)

                def sD():
                    act(rs_.v(), pair(2), AF.Ln, bias=epsc)
                    act(rs_.v(), rs_.v(), AF.Exp, scale=-0.5)

                def sE():
                    stt(qn_.v(), pair(b_), gcol, rs_.v(), ALU.mult, ALU.mult)

                def sF():
                    for i in range(2):
                        mm(pair(3, i * 512, i * 512 + 512), rotT, qn_.v(i * 512, i * 512 + 512), True, True)

                def sG():
                    tt(t1.v(), qn_.v(), cst.v(C_COS + c0, C_COS + c0 + K1), ALU.mult)
                    tt(qn_.v(), pair(3), cst.v(C_SIN + c0, C_SIN + c0 + K1), ALU.mult)
                    tt(dst.v(c0, c0 + K1), t1.v(), qn_.v(), ALU.add)
                return [sA, sB, sC, sD, sE, sF, sG]

            P.phase = f'L{l}.kv'
            rstate["pos"] = 0
            slab_kv = wslab(w_in_d[l, :, 512:768], KC, 256)
            slab_q = [wslab(w_in_d[l, :, i * 256:(i + 1) * 256], KC, 256) for i in range(2)]
            rstate["lo"] = rstate["pos"]
            assert rstate["lo"] == 6144
            memset(Vaug.v(), 1.0)
            for t in range(16):
                for kc in range(KC):
                    mm(ps.v(2048 + t * 128, 2048 + t * 128 + 128), hT.v(kc * S + t * 128, kc * S + t * 128 + 128),
                       slab_kv(kc, 128, 256), kc == 0, kc == KC - 1)
            vsrc = grp(1)
            vsrc.ap = vsrc.ap.rearrange("p (t g c) -> p t g c", t=16, g=2)
            vdst = Vaug.v()
            vdst.ap = vdst.ap.rearrange("p (t g c) -> p t g c", t=16, g=2)[:, :, :, 64:128]
            cp(vdst, vsrc)
            jobs = []
            for half in range(2):
                jobs.append(qk_stages(len(jobs), slab_kv, 0, pcol(O_KG), kTa, half))
            for c in range(4):
                for half in range(2):
                    jobs.append(qk_stages(len(jobs), slab_q[c // 2], (c % 2) * 128, sm.v(mb + 64, mb + 65), qbuf[c], half))
            SKEW = 4
            for t in range(len(jobs) * SKEW + 7):
                for j, job in enumerate(jobs):
                    st_ = t - j * SKEW
                    if 0 <= st_ < 7:
                        job[st_]()
                if t == 1 * SKEW + 7:
                    pass
            Z11 = carve(SCR + 24576, S, BF16)
            Z01 = carve(SCR + 28672, S, BF16)
            for tb in range(4):
                mm(bank(tb), swapb, kTa.v(tb * 512, tb * 512 + 512), True, True)
            cp(Z11.v(0, S, 64, 128), kTa.v(0, S, 64, 128))
            memset(Z11.v(0, S, 0, 64), 0.0)
            memset(Z01.v(0, S, 0, 64), 0.0)
            act(kTb.v(0, S, 0, 64), ps.v(0, 2048, 0, 64), AF.Copy)
            act(Z01.v(0, S, 64, 128), ps.v(0, 2048, 64, 128), AF.Copy)
            memset(kTa.v(0, S, 64, 128), 0.0)
            memset(kTb.v(0, S, 64, 128), 0.0)
            kz = [[kTa, Z01], [kTb, Z11]]

            P.phase = f'L{l}.attn'
            accsb = rstd_q
            mods_pending = [(l + 1, s_, 0) for s_ in range(24)] if l + 1 < n_layers else []
            if l == 0:
                mods_pending = [(0, s_, 64) for s_ in range(8, 24)] + mods_pending
            mod_every = max(2, 256 // (len(mods_pending) + 1))
            mod_halves = []
            mods_done = False

            def mod_fin(l=l, mb=mb):
                if l == 0:
                    mod_finish(0, mb, 16, 48, 64)
                if l + 1 < n_layers:
                    mod_finish(l + 1, ((l + 1) % 2) * 80)
            gstep = 0

            def s_emit(qT, g, st, idx):
                par, qh, kb = st
                kX = kz[g][par]
                for i in range(2):
                    mm(pair(idx % 2, i * 512, i * 512 + 512),
                       kX.v(kb * 128, kb * 128 + 128),
                       qT.v(qh * K1 + i * 512, qh * K1 + i * 512 + 512), True, True)

            steps = [(c, par, qh, kb) for c in range(4) for par in range(2) for qh in range(2) for kb in range(16)]
            NS = len(steps)
            s_emit(qbuf[steps[0][0]], steps[0][0] // 2, steps[0][1:], 0)
            s_emit(qbuf[steps[1][0]], steps[1][0] // 2, steps[1][1:], 1)
            for si, (c, par, qh, kb) in enumerate(steps):
                g = c // 2
                E = Es[si % 3]
                act(E.v(), pair(si % 2), AF.Exp)
                if si + 2 < NS:
                    c2 = steps[si + 2][0]
                    s_emit(qbuf[c2], c2 // 2, steps[si + 2][1:], si + 2)
                vlo = 0 if par == 1 else 64
                vb = (kb * 2 + g) * 192 + vlo
                for i in range(2):
                    mm(pair(2, i * 512, i * 512 + 512), Vaug.v(vb, vb + 128),
                       E.v(i * 512, i * 512 + 512), kb == 0, kb == 15)
                if si < 8:
                    mm(bank(7), identb, hT.v(0, 512), True, True)
                if kb == 15:
                    orow0, drow0 = (0, 64) if par == 0 else (64, 0)
                    cp(accsb.v(0, 512), bank(4))
                    cp(accsb.v(512, K1), bank(5))
                    rd = rden.v(0, K1, orow0, orow0 + 64)
                    recip(rd, accsb.v(0, K1, drow0, drow0 + 64))
                    tt(attnT.v(c * S + qh * K1, c * S + qh * K1 + K1, orow0, orow0 + 64),
                       accsb.v(0, K1, orow0, orow0 + 64), rd, ALU.mult)
                gstep += 1
                if mod_halves:
                    mod_halves.pop(0)()
                elif mods_pending and gstep % mod_every == mod_every // 2:
                    mod_halves = mod_slab_mm(*mods_pending.pop(0))
                    mod_halves.pop(0)()
                if not mods_done and not mods_pending and not mod_halves:
                    mods_done = True
                    mod_fin()
            while mods_pending or mod_halves:
                if not mod_halves:
                    mod_halves = mod_slab_mm(*mods_pending.pop(0))
                mod_halves.pop(0)()
            if not mods_done:
                mod_fin()
            rstate["lo"] = 0

            P.phase = f'L{l}.conv'
            sbuf_ = carve(SCR, S, BF16)
            UW = S + 32
            us = [carve(SCR + 4096 + i * (UW * 2), UW, BF16) for i in range(2)]
            diags = [carve(SCR + 4096 + 2 * UW * 2 + i * 7936, 31 * 128, BF16) for i in range(2)]
            TD = 8
            cacc = carve(SCR + 4096 + 2 * UW * 2 + 2 * 7936, S, F32)
            slab_a = [None, None]
            slab_b = [None, None]
            for c in range(4):
                if c % 2 == 0:
                    slab_a[0] = wslab(w_in_d[l, :, 768 + (c // 2) * 256: 768 + (c // 2) * 256 + 256], KC, 256)
                    slab_b[0] = wslab(w_in_d[l, :, 1280 + (c // 2) * 256: 1280 + (c // 2) * 256 + 256], KC, 256)
                u = us[c % 2]
                dg = diags[c % 2]
                for j in range(TD, 31):
                    tsm(dg.v(j * 128, j * 128 + 128), identb, pcol(O_DW + c * 31 + j))
                memset(u.v(0, 16), 0.0)
                memset(u.v(16 + S, UW), 0.0)
                proj(lambda tb: bank(4 + tb), slab_b[0], (c % 2) * 128, KC, hT_rhs)
                for th in range(2):
                    act(sbuf_.v(th * K1, th * K1 + K1), pair(2 + th), AF.Sigmoid)
                proj(lambda tb: bank(tb), slab_a[0], (c % 2) * 128, KC, hT_rhs)
                for th in range(2):
                    tt(u.v(16 + th * K1, 16 + th * K1 + K1), pair(th), sbuf_.v(th * K1, th * K1 + K1), ALU.mult)
                for j in range(TD):
                    src = u.v(j + 1, j + 1 + S)
                    wj = pcol(O_DW + c * 31 + j)
                    if j == 0:
                        tsm(cacc.v(), src, wj)
                    else:
                        stt(cacc.v(), src, wj, cacc.v(), ALU.mult, ALU.add)
                for tb in range(4):
                    for j in range(TD, 31):
                        mm(bank(4 + tb), dg.v(j * 128, j * 128 + 128), u.v(tb * 512 + j + 1, tb * 512 + j + 1 + 512), j == TD, j == 30)
                stt(ucn.v(c * S, (c + 1) * S), grp(1), pcol(O_DWB + c), cacc.v(), ALU.add, ALU.add)
            P.phase = f'L{l}.ln'
            sqs = [carve(SCR + i * 4096, S, BF16) for i in range(2)]
            m2 = carve(SCR + 8192, S, F32)
            tmps = [carve(SCR + 16384 + i * 8192, S, F32) for i in range(2)]
            for c in range(4):
                sq = sqs[c % 2]
                act(sq.v(), ucn.v(c * S, (c + 1) * S), AF.Square)
                for tb in range(4):
                    mm(bank(tb), onesc, ucn.v(c * S + tb * 512, c * S + tb * 512 + 512), c == 0, c == 3)
                for tb in range(4):
                    mm(bank(4 + tb), onesc, sq.v(tb * 512, tb * 512 + 512), c == 0, c == 3)
            act(m2.v(), grp(0), AF.Square)
            meanb = sqs[0]
            act(meanb.v(), grp(0), AF.Copy)
            tt(m2.v(), grp(1), m2.v(), ALU.subtract)
            act(m2.v(), m2.v(), AF.Ln, bias=epsc)
            act(m2.v(), m2.v(), AF.Exp, scale=-0.5)
            for c in range(4):
                gi_ = (c + 1) % 2
                for tb in range(4):
                    mm(bank(gi_ * 4 + tb), identb, ucn.v(c * S + tb * 512, c * S + tb * 512 + 512), True, False)
                    mm(bank(gi_ * 4 + tb), nidentb, meanb.v(tb * 512, tb * 512 + 512), False, True)
                tmp = tmps[c % 2]
                tt(tmp.v(), grp(gi_), m2.v(), ALU.mult)
                act(ucn.v(c * S, (c + 1) * S), tmp.v(), AF.Silu, bias=pcol(O_LNB + c), scale=pcol(O_LNG + c))

            P.phase = f'L{l}.merge'
            sgc = carve(SCR, S, BF16)
            sga = carve(SCR + 4096, S, BF16)
            mbuf = carve(SCR + 8192, S, BF16)
            tbuf = carve(SCR + 12288, S, F32)
            merged = carve(SCR + 20480, 4 * S, BF16)
            gcount = 0
            for hj in range(2):
                for jj in range(4):
                    j = hj * 4 + jj
                    if j % 2 == 0:
                        s_gc = wslab(w_in_d[l, :, 1792 + (j // 2) * 256: 1792 + (j // 2) * 256 + 256], KC, 256)
                        s_co = wslab(w_co_d[l, :, (j // 2) * 256:(j // 2) * 256 + 256], 4, 256)
                        s_ga = wslab(w_in_d[l, :, 2816 + (j // 2) * 256: 2816 + (j // 2) * 256 + 256], KC, 256)
                        s_ao = wslab(w_ao_d[l, :, (j // 2) * 256:(j // 2) * 256 + 256], 4, 256)
                    n0 = (j % 2) * 128
                    proj(lambda tb: bank(4 + tb), s_gc, n0, KC, hT_rhs)
                    act(sgc.v(), grp(1), AF.Sigmoid)
                    proj(lambda tb: bank(tb), s_co, n0, 4,
                         lambda kc, tb: ucn.v(kc * S + tb * 512, kc * S + tb * 512 + 512))
                    stt(mbuf.v(), grp(0), pcol(O_BCO + j), sgc.v(), ALU.add, ALU.mult)
                    proj(lambda tb: bank(4 + tb), s_ga, n0, KC, hT_rhs)
                    act(sga.v(), grp(1), AF.Sigmoid)
                    proj(lambda tb: bank(tb), s_ao, n0, 4,
                         lambda kc, tb: attnT.v(kc * S + tb * 512, kc * S + tb * 512 + 512))
                    tt(tbuf.v(), grp(0), sga.v(), ALU.mult)
                    tt(merged.v(jj * S, (jj + 1) * S), tbuf.v(), mbuf.v(), ALU.add)
                if hj == 1:
                    nb2 = NB2()
                for n in range(8):
                    if n % 2 == 0:
                        s_wo = wslab(w_out_d[l, hj * 512:(hj + 1) * 512, (n // 2) * 256:(n // 2) * 256 + 256], 4, 256)
                    def wo_mm(out_fn, n_, kcs, rhs_fn, tbs):
                        for kc in kcs:
                            w = s_wo(kc, (n_ % 2) * 128, (n_ % 2) * 128 + 128)
                            for tb in tbs:
                                mm(out_fn(tb), w, rhs_fn(kc, tb), kc == 0, kc == 3)
                    if hj == 0:
                        rhs_ = lambda kc, tb: merged.v(kc * S + tb * 512, kc * S + tb * 512 + 512)
                        gof = lambda n_: (lambda tb: bank(((n_ + 1) % 2) * 4 + tb))
                        if n == 0:
                            wo_mm(gof(0), 0, (0, 1, 2), rhs_, range(4))
                            wo_mm(gof(1), 1, (0, 1, 2), rhs_, range(4))
                            for n_ in (0, 1):
                                wo_mm(gof(n_), n_, (3,), rhs_, range(4))
                                stt(xT.v(n_ * S, (n_ + 1) * S), grp((n_ + 1) % 2), modT(GATE_M + n_),
                                    xT.v(n_ * S, (n_ + 1) * S), ALU.mult, ALU.add)
                        elif n >= 2:
                            wo_mm(gof(n), n, (0, 1, 2, 3), rhs_, range(4))
                            stt(xT.v(n * S, (n + 1) * S), grp((n + 1) % 2), modT(GATE_M + n),
                                xT.v(n * S, (n + 1) * S), ALU.mult, ALU.add)
                    else:
                        rh_ = lambda th: (lambda kc, tb: merged.v(kc * S + (th * 2 + tb) * 512, kc * S + (th * 2 + tb) * 512 + 512))
                        po_ = lambda th: (lambda tb: pair(2 + th, tb * 512, tb * 512 + 512))
                        if n == 0:
                            for th in range(2):
                                wo_mm(po_(th), n, (0, 1, 2), rh_(th), range(2))
                        for th in range(2):
                            wo_mm(po_(th), n, (3,) if n == 0 else (0, 1, 2, 3), rh_(th), range(2))
                            xs_ = xT.v(n * S + th * K1, n * S + th * K1 + K1)
                            stt(xs_, pair(2 + th), modT(GATE_M + n), xs_, ALU.mult, ALU.add)
                        if n >= 1:
                            norm_chunk(nb2, n - 1)
                if hj == 1:
                    norm_chunk(nb2, 7)

            P.phase = f'L{l}.norm2'
            norm_finish(nb2, mb + 56, mb + SH_F)
            P.phase = f'L{l}.ffn'
            actb = carve(0, 12 * S, BF16)
            sgs = [carve(12 * S * 2 + i * 4096, S, BF16) for i in range(2)]
            f0 = 0
            for hf, nf in enumerate((12, 10)):
                for ff in range(nf):
                    f = f0 + ff
                    if f % 2 == 0:
                        s_g = wslab(w_fi_d[l, :, (f // 2) * 256:(f // 2) * 256 + 256], KC, 256)
                        s_u = wslab(w_fi_d[l, :, DFF + (f // 2) * 256: DFF + (f // 2) * 256 + 256], KC, 256)
                    sg = sgs[f % 2]
                    proj(lambda tb: bank(tb), s_g, (f % 2) * 128, KC, hT_rhs)
                    act(sg.v(), grp(0), AF.Silu)
                    proj(lambda tb: bank(4 + tb), s_u, (f % 2) * 128, KC, hT_rhs)
                    tt(actb.v(ff * S, (ff + 1) * S), grp(1), sg.v(), ALU.mult)
                ovl = (hf == 1 and l + 1 < n_layers)
                if ovl:
                    nb1 = NB1()
                for n in range(8):
                    s_fo = wslab(w_fo_d[l, f0 * 128:(f0 + nf) * 128, n * 128:(n + 1) * 128], nf, 128)
                    if not ovl:
                        gi = n % 2
                        proj(lambda tb: bank(gi * 4 + tb), s_fo, 0, nf,
                             lambda kc, tb: actb.v(kc * S + tb * 512, kc * S + tb * 512 + 512))
                        stt(xT.v(n * S, (n + 1) * S), grp(gi), modT(GATE_F + n), xT.v(n * S, (n + 1) * S), ALU.mult, ALU.add)
                    else:
                        for th in range(2):
                            proj(lambda tb: pair(th, tb * 512, tb * 512 + 512), s_fo, 0, nf,
                                 lambda kc, tb: actb.v(kc * S + (th * 2 + tb) * 512, kc * S + (th * 2 + tb) * 512 + 512),
                                 tbs=range(2))
                            xs_ = xT.v(n * S + th * K1, n * S + th * K1 + K1)
                            stt(xs_, pair(th), modT(GATE_F + n), xs_, ALU.mult, ALU.add)
                        if n >= 1:
                            norm_chunk(nb1, n - 1)
                if ovl:
                    norm_chunk(nb1, 7)
                f0 += nf

        P.phase = 'final'
        fg = carve(0, D, F32)
        dma("sp", fg.v(), fg_d, "fg")
        junk = carve(4096, D, F32)
        ostg = [carve(8192 + i * 4096, D, F32) for i in range(2)]
        stat = carve(16384, 64, F32)
        memset(stat.v(), 0.0)
        outs = []
        for t in range(16):
            pr = t % 4
            for kc in range(KC):
                tr(pair(pr, kc * 128, kc * 128 + 128), xT.v(kc * S + t * 128, kc * S + t * 128 + 128), ident)
            ss = stat.v(2 * t, 2 * t + 1)
            sd = stat.v(2 * t + 1, 2 * t + 2)
            act(junk.v(), pair(pr), AF.Square, accum=ss)
            act(sd, ss, AF.Sqrt, bias=epsc, scale=1.0 / D)
            recip(sd, sd)
            o = ostg[t % 2]
            stt(o.v(), pair(pr), sd, fg.v(), ALU.mult, ALU.mult)
            outs.append(dma_out("sp", out_d[t * 128:(t + 1) * 128, :], o.v(), f"o{t % 2}"))
        if dbg and "sm" in dbg_d:
            outs.append(dma_out("sp", dbg_d["sm"], sm.v(), "dbgsm"))
            outs = outs[-3:]
        else:
            outs = outs[-2:]
        P.emit(ctx, final_waits=outs)
        build.stats = {k: len(v) for k, v in P.ops.items()}
        build.stats["sems"] = P.nsems
        build.prog = P
    return nc


def _consts():
    cst = np.zeros((128, NCST), np.float32)
    cst[:, C_ID:C_ID + 128] = np.eye(128, dtype=np.float32)
    R = np.zeros((128, 128), np.float32)
    for p in range(128):
        i = p % 32
        if i < 16:
            R[p, p + 16] = -1.0
        else:
            R[p, p - 16] = 1.0
    cst[:, C_ROT:C_ROT + 128] = R.T
    tpos = np.arange(S)
    row = (tpos // 64).astype(np.float32)
    col = (tpos % 64).astype(np.float32)
    inv = (np.float32(10000.0) ** (-np.arange(0, 32, 2, dtype=np.float32) / np.float32(32))).astype(np.float32)
    for p in range(128):
        i = p % 16
        seg = (p % 64) // 32
        pos = row if seg == 0 else col
        ang = (pos * inv[i]).astype(np.float32)
        cst[p, C_COS:C_COS + S] = np.cos(ang)
        cst[p, C_SIN:C_SIN + S] = np.sin(ang)
    cst[:, C_EPS] = EPS
    cst[:, C_ONE] = 1.0
    cb = np.zeros((128, NCSTB), np.float32)
    cb[:, B_ONESD:B_ONESD + 128] = 1.0 / D
    blk = np.zeros((128, 128), np.float32)
    blk[:64, :64] = 1.0 / 64
    blk[64:, 64:] = 1.0 / 64
    cb[:, B_BLK:B_BLK + 128] = blk
    cb[:, B_ONESC:B_ONESC + 128] = 1.0 / 512
    cb[:, B_ID:B_ID + 128] = np.eye(128, dtype=np.float32)
    sw = np.zeros((128, 128), np.float32)
    for p in range(128):
        sw[(p + 64) % 128, p] = 1.0
    cb[:, B_SWAP:B_SWAP + 128] = sw
    cb[:, B_NID:B_NID + 128] = -np.eye(128, dtype=np.float32)
    return cst, cb


def _pack(b, c, b_ada, norm_mix_g, q_norm_g, k_norm_g, conv_dw, conv_dw_b, conv_ln_g, conv_ln_b,
          b_conv_o, norm_ffn_g):
    pk = np.zeros((128, NPK), np.float32)
    col = lambda v: np.ascontiguousarray(v.reshape(-1, 128).T)
    pk[:, PK_C:PK_C + 8] = col(c[b])
    for l in range(DEPTH):
        o = PK_L0 + l * LP
        pk[:, o + O_GMIX:o + O_GMIX + 8] = col(norm_mix_g[l])
        pk[:, o + O_GFFN:o + O_GFFN + 8] = col(norm_ffn_g[l])
        pk[:, o + O_QG] = np.tile(q_norm_g[l], 2)
        pk[:, o + O_KG] = np.tile(k_norm_g[l], 2)
        dw = conv_dw[l].reshape(31, 4, 128).transpose(2, 1, 0).reshape(128, 124)
        pk[:, o + O_DW:o + O_DW + 124] = dw
        pk[:, o + O_DWB:o + O_DWB + 4] = col(conv_dw_b[l])
        pk[:, o + O_LNG:o + O_LNG + 4] = col(conv_ln_g[l])
        pk[:, o + O_LNB:o + O_LNB + 4] = col(conv_ln_b[l])
        pk[:, o + O_BCO:o + O_BCO + 8] = col(b_conv_o[l])
        pk[:, o + O_BADA:o + O_BADA + 48] = col(b_ada[l])
    return pk


_NC_CACHE = {}


def make_in_maps(cores, x, c, w_ada, b_ada, norm_mix_g, w_in, q_norm_g, k_norm_g, w_attn_o,
                 conv_dw, conv_dw_b, conv_ln_g, conv_ln_b, w_conv_o, b_conv_o, w_out,
                 norm_ffn_g, w_ffn_in, w_ffn_out, final_norm_g):
    f = lambda a: np.ascontiguousarray(np.asarray(a, dtype=np.float32))
    cst, cb = _consts()
    fg = np.ascontiguousarray(np.broadcast_to(f(final_norm_g)[None, :], (128, D)))
    shared = {"cst": cst, "cstb": cb, "fg": fg, "w_ada": f(w_ada), "w_in": f(w_in), "w_attn_o": f(w_attn_o),
              "w_conv_o": f(w_conv_o), "w_out": f(w_out), "w_ffn_in": f(w_ffn_in), "w_ffn_out": f(w_ffn_out)}
    x = f(x)
    maps = []
    for b in cores:
        m = dict(shared)
        m["x"] = np.ascontiguousarray(x[b])
        m["pk"] = _pack(b, f(c), f(b_ada), f(norm_mix_g), f(q_norm_g), f(k_norm_g), f(conv_dw), f(conv_dw_b),
                        f(conv_ln_g), f(conv_ln_b), f(b_conv_o), f(norm_ffn_g))
        maps.append(m)
    return maps


def kernel(**inputs):
    if "nc" not in _NC_CACHE:
        _NC_CACHE["nc"] = build(DEPTH)
    nc = _NC_CACHE["nc"]
    maps = make_in_maps(list(range(NCORES)), **inputs)
    res = run_bass_kernel_spmd(nc, maps, core_ids=list(range(NCORES)))
    out = np.stack([np.asarray(r["out"], dtype=np.float32) for r in res.results], axis=0)
    return out
```

```python
import contextlib
import numpy as np
import concourse.bass as bass
import concourse.mybir as mybir
from concourse.bass_utils import run_bass_kernel_spmd

F32 = mybir.dt.float32
BF16 = mybir.dt.bfloat16
F32R = mybir.dt.float32r
AF = mybir.ActivationFunctionType
ALU = mybir.AluOpType

S = 2048
D = 1024
KC = 8
DEPTH = 4
DFF = 2816
INW = 3840
EPS = 1e-6
NCORES = 8

COMPUTE = ("pe", "act", "dve", "pool")

LP = 210
PK_C = 0
PK_L0 = 8
O_GMIX, O_GFFN, O_QG, O_KG, O_DW, O_DWB, O_LNG, O_LNB, O_BCO, O_BADA = 0, 8, 16, 17, 18, 142, 146, 150, 154, 162
NPK = PK_L0 + DEPTH * LP
C_ID, C_ROT, C_COS, C_SIN, C_EPS, C_ONE = 0, 128, 256, 2304, 4352, 4353
NCST = 4354
B_ONESD, B_BLK, B_ONESC, B_ID, B_SWAP, B_NID = 0, 128, 256, 384, 512, 640
NCSTB = 768


class View:
    __slots__ = ("ap", "regs")

    def __init__(self, ap, regs):
        self.ap = ap
        self.regs = regs


class Buf:
    def __init__(self, key, ap2d, esize, byte_off=0, excl=False):
        self.key = key
        self.ap = ap2d
        self.esize = esize
        self.off = byte_off
        self.excl = excl
        self.n = ap2d.shape[1]

    def _reg(self, c0, c1):
        lo = self.off + c0 * self.esize
        hi = self.off + c1 * self.esize
        if self.excl:
            lo = (lo // 2048) * 2048
            hi = ((hi + 2047) // 2048) * 2048
        return (self.key, lo, hi, self.excl)

    def v(self, c0=None, c1=None, p0=None, p1=None):
        c0 = 0 if c0 is None else c0
        c1 = self.n if c1 is None else c1
        assert 0 <= c0 < c1 <= self.n, (self.key, c0, c1, self.n)
        ap = self.ap[:, c0:c1] if p0 is None else self.ap[p0:p1, c0:c1]
        return View(ap, [self._reg(c0, c1)])

    def v3(self, c0, c1, a, p0=None, p1=None):
        vw = self.v(c0, c1, p0, p1)
        vw.ap = vw.ap.rearrange("p (a b) -> p a b", a=a)
        return vw

    def vs(self, a, t0, t1):
        b = self.n // a
        ap = self.ap.rearrange("p (a b) -> p a b", a=a)[:, :, t0:t1]
        return View(ap, [self._reg(i * b + t0, i * b + t1) for i in range(a)])


class Op:
    __slots__ = ("eng", "fn", "deps", "marked", "sig", "idx", "is_dma", "slot", "phase")


class Prog:
    def __init__(self, nc):
        self.nc = nc
        self.ops = {k: [] for k in ("pe", "act", "dve", "pool", "sp")}
        self.seg = {}
        self.allops = []
        self.phase = ''

    def _read(self, op, key, lo, hi, deps):
        for s in self.seg.get(key, ()):
            if s[0] < hi and lo < s[1]:
                if s[2] is not None:
                    deps.append(s[2])
                s[3][(op.eng, op.is_dma and id(op))] = op

    def _write(self, op, key, lo, hi, deps):
        segs = self.seg.get(key, [])
        new = []
        for s in segs:
            if s[0] < hi and lo < s[1]:
                if s[2] is not None:
                    deps.append(s[2])
                deps.extend(s[3].values())
                if s[0] < lo:
                    new.append([s[0], lo, s[2], dict(s[3])])
                if hi < s[1]:
                    new.append([hi, s[1], s[2], dict(s[3])])
            else:
                new.append(s)
        new.append([lo, hi, op, {}])
        self.seg[key] = new

    def op(self, eng, fn, r=(), w=(), dma=False, slot=None):
        o = Op()
        o.eng = eng
        o.fn = fn
        o.is_dma = dma
        o.slot = slot
        o.marked = False
        o.sig = None
        o.phase = self.phase
        deps = []
        for vw in r:
            for (key, lo, hi, excl) in vw.regs:
                if excl:
                    self._write(o, key, lo, hi, deps)
                else:
                    self._read(o, key, lo, hi, deps)
        for vw in w:
            for (key, lo, hi, excl) in vw.regs:
                self._write(o, key, lo, hi, deps)
        best = {}
        out = []
        for d in deps:
            if d is o:
                continue
            if d.is_dma:
                if d not in out:
                    out.append(d)
            else:
                if d.eng == "pe" and eng == "pe" and not dma:
                    continue
                b = best.get(d.eng)
                if b is None or d.idx > b.idx:
                    best[d.eng] = d
        out.extend(best.values())
        for d in out:
            d.marked = True
        o.deps = out
        lst = self.ops[eng]
        o.idx = len(lst)
        lst.append(o)
        self.allops.append(o)
        return o

    def emit(self, ctx, final_waits=()):
        nc = self.nc
        sems = {}
        counts = {}
        CH = 4000

        def getsem(name):
            if name not in sems:
                sems[name] = ctx.enter_context(nc.semaphore(name))
                counts[name] = 0
            return sems[name]

        for eng in COMPUTE:
            n = 0
            for o in self.ops[eng]:
                if o.is_dma:
                    continue
                if o.marked:
                    sname = f"s_{eng}_{n // CH}"
                    s = getsem(sname)
                    counts[sname] += 1
                    o.sig = (s, counts[sname], sname)
                    n += 1
        for o in self.allops:
            if o.is_dma:
                sname = f"d_{o.slot}"
                s = getsem(sname)
                counts[sname] += 16
                o.sig = (s, counts[sname], sname)
        self.nsems = len(sems)

        def run(eng, e):
            waited = {}
            for o in self.ops[eng]:
                for d in o.deps:
                    s, val, sname = d.sig
                    if waited.get(sname, 0) >= val:
                        continue
                    e.wait_ge(s, val)
                    waited[sname] = val
                ins = o.fn(e)
                if o.is_dma:
                    ins.then_inc(o.sig[0], 16)
                elif o.marked:
                    ins.then_inc(o.sig[0], 1)
            if eng == "sp":
                for o in final_waits:
                    s, val, sname = o.sig
                    e.wait_ge(s, val)

        with nc.Block() as block:
            @block.sync
            def _(e):
                run("sp", e)

            @block.gpsimd
            def _(e):
                run("pool", e)

            @block.scalar
            def _(e):
                run("act", e)

            @block.vector
            def _(e):
                run("dve", e)

            @block.tensor
            def _(e):
                run("pe", e)


def build(n_layers=DEPTH, dbg=None):
    nc = bass.Bass("TRN2", target_bir_lowering=False)
    dr = lambda name, shape: nc.dram_tensor(name, shape, F32, kind="ExternalInput").ap()
    x_d = dr("x", [S, D])
    pk_d = dr("pk", [128, NPK])
    cst_d = dr("cst", [128, NCST])
    cstb_d = dr("cstb", [128, NCSTB])
    fg_d = dr("fg", [128, D])
    w_ada_d = dr("w_ada", [DEPTH, D, 6 * D])
    w_in_d = dr("w_in", [DEPTH, D, INW])
    w_ao_d = dr("w_attn_o", [DEPTH, 512, D])
    w_co_d = dr("w_conv_o", [DEPTH, 512, D])
    w_out_d = dr("w_out", [DEPTH, D, D])
    w_fi_d = dr("w_ffn_in", [DEPTH, D, 2 * DFF])
    w_fo_d = dr("w_ffn_out", [DEPTH, DFF, D])
    out_d = nc.dram_tensor("out", [S, D], F32, kind="ExternalOutput").ap()
    dbg_d = {}
    if dbg:
        for name, shape in dbg.items():
            dbg_d[name] = nc.dram_tensor("dbg_" + name, shape, F32, kind="ExternalOutput").ap()

    ARENA_B = 68 * 1024
    RING_COLS = 10 * 1024
    with contextlib.ExitStack() as ctx:
        sb = lambda name, cols, dt: ctx.enter_context(nc.sbuf_tensor("sb_" + name, [128, cols], dt))
        xT_t = sb("xT", KC * S, F32)
        hT_t = sb("hT", KC * S, BF16)
        cst_t = sb("cstf", NCST, F32)
        cstb_t = sb("cstb", NCSTB, BF16)
        pk_t = sb("pk", NPK, F32)
        sm_t = sb("small", 256, F32)
        smb_t = sb("smallb", 16, BF16)
        ring_t = sb("ring", RING_COLS, BF16)
        arena_t = sb("arena", ARENA_B // 2, BF16)
        ps_t = ctx.enter_context(nc.psum_tensor("ps", [128, 4096], F32))

        P = Prog(nc)
        xT = Buf("xT", xT_t[:, :], 4)
        hT = Buf("hT", hT_t[:, :], 2)
        cst = Buf("cst", cst_t[:, :], 4)
        cstb = Buf("cstb", cstb_t[:, :], 2)
        pk = Buf("pk", pk_t[:, :], 4)
        sm = Buf("sm", sm_t[:, :], 4)
        smb = Buf("smb", smb_t[:, :], 2)
        ring = Buf("ring", ring_t[:, :], 2)
        ps = Buf("psum", ps_t[:, :], 4, 0, excl=True)

        def carve(off, ncols, dt):
            es = 4 if dt == F32 else 2
            assert off % 4 == 0 and off + ncols * es <= ARENA_B, (off, ncols, es)
            ap = arena_t[:, off // 2: off // 2 + ncols * es // 2]
            if dt == F32:
                ap = ap.bitcast(F32)
            return Buf("arena", ap, es, off)

        K1 = 1024
        cact = smb.v(0, 8)

        def mm(out, lhsT, rhs, start, stop):
            P.op("pe", lambda e: e.matmul(out.ap, lhsT=lhsT.ap, rhs=rhs.ap, start=start, stop=stop),
                 r=[lhsT, rhs], w=[out])

        def tr(out, in_, ident):
            P.op("pe", lambda e: e.transpose(out.ap, in_.ap, ident.ap), r=[in_, ident], w=[out])

        def act(out, in_, func, bias=None, scale=None, accum=None):
            r = [in_]
            kw = {}
            if bias is not None:
                kw["bias"] = bias.ap
                r.append(bias)
            if scale is not None:
                if isinstance(scale, View):
                    kw["scale"] = scale.ap
                    r.append(scale)
                else:
                    kw["scale"] = scale
            w = [out]
            if accum is not None:
                kw["accum_out"] = accum.ap
                w.append(accum)
            P.op("act", lambda e: e.activation(out=out.ap, in_=in_.ap, func=func, **kw), r=r, w=w)

        def tt(out, a, b, op, eng="dve"):
            P.op(eng, lambda e: e.tensor_tensor(out=out.ap, in0=a.ap, in1=b.ap, op=op), r=[a, b], w=[out])

        def stt(out, a, scalar, b, op0, op1):
            r = [a, b]
            if isinstance(scalar, View):
                r.append(scalar)
                sc = scalar.ap
            else:
                sc = scalar
            P.op("dve", lambda e: e.scalar_tensor_tensor(out=out.ap, in0=a.ap, scalar=sc, in1=b.ap, op0=op0, op1=op1),
                 r=r, w=[out])

        def tsm(out, a, scalar):
            r = [a]
            if isinstance(scalar, View):
                r.append(scalar)
                sc = scalar.ap
            else:
                sc = scalar
            P.op("dve", lambda e: e.tensor_scalar(out=out.ap, in0=a.ap, scalar1=sc, scalar2=None, op0=ALU.mult),
                 r=r, w=[out])

        def recip(out, a):
            P.op("dve", lambda e: e.reciprocal(out=out.ap, in_=a.ap), r=[a], w=[out])

        def cp(out, a, eng="dve"):
            P.op(eng, lambda e: e.tensor_copy(out=out.ap, in_=a.ap), r=[a], w=[out])

        def memset(out, val, eng="dve"):
            P.op(eng, lambda e: e.memset(out.ap, val), w=[out])

        def dma(eng, out, in_ap, slot, r=()):
            return P.op(eng, lambda e: e.dma_start(out=out.ap, in_=in_ap), r=list(r), w=[out], dma=True, slot=slot)

        def dma_out(eng, out_ap, in_, slot):
            return P.op(eng, lambda e: e.dma_start(out=out_ap, in_=in_.ap), r=[in_], dma=True, slot=slot)

        rstate = {"pos": 0, "n": 0, "lo": 0, "hi": RING_COLS}

        def wslab(src_ap, kcn, ncols):
            tot = kcn * ncols
            if rstate["pos"] + tot > rstate["hi"] or rstate["pos"] < rstate["lo"]:
                rstate["pos"] = rstate["lo"]
            c0 = rstate["pos"]
            rstate["pos"] += tot
            i = rstate["n"]
            rstate["n"] += 1
            dst = ring.v3(c0, c0 + tot, kcn)
            dma("pool", dst, src_ap.rearrange("(kc p) n -> p kc n", p=128), f"w{i % 16}")
            return lambda kc, a, b: ring.v(c0 + kc * ncols + a, c0 + kc * ncols + b)

        bank = lambda i, a=0, b=512: ps.v(i * 512 + a, i * 512 + b)
        pair = lambda i, a=0, b=1024: ps.v(i * 1024 + a, i * 1024 + b)
        grp = lambda i, a=0, b=2048: ps.v(i * 2048 + a, i * 2048 + b)

        SCR = 32 * 1024

        dma("sp", pk.v(), pk_d, "pk")
        dma("sp", cst.v(), cst_d, "cst")
        dma("pool", cstb.v(), cstb_d, "cstb")
        ident = cst.v(C_ID, C_ID + 128)
        rotT = cst.v(C_ROT, C_ROT + 128)
        epsc = cst.v(C_EPS, C_EPS + 1)
        onesd = cstb.v(B_ONESD, B_ONESD + 128)
        blk64 = cstb.v(B_BLK, B_BLK + 128)
        onesc = cstb.v(B_ONESC, B_ONESC + 128)
        identb = cstb.v(B_ID, B_ID + 128)
        swapb = cstb.v(B_SWAP, B_SWAP + 128)
        nidentb = cstb.v(B_NID, B_NID + 128)

        P.phase = 'xload'
        for t in range(16):
            stg = carve(SCR + (t % 4) * 4096, 1024, F32)
            dma("sp", stg.v(), x_d[t * 128:(t + 1) * 128, :], f"xs{t % 4}")
            pr = t % 4
            for kc in range(KC):
                tr(pair(pr, kc * 128, kc * 128 + 128), stg.v(kc * 128, kc * 128 + 128), ident)
            src = pair(pr)
            src.ap = src.ap.rearrange("p (a b) -> p a b", a=KC)
            dstv = xT.vs(KC, t * 128, (t + 1) * 128)
            if t % 2 == 0:
                P.op("act", lambda e, o=dstv, i=src: e.activation(out=o.ap, in_=i.ap, func=AF.Copy), r=[src], w=[dstv])
            else:
                cp(dstv, src)

        act(cact, pk.v(PK_C, PK_C + 8), AF.Silu)

        def norm_bufs(sq_off, rstd_off, tmp_offs, sg=1):
            return dict(sqs=[carve(sq_off + i * 4096, S, BF16) for i in range(2)],
                        rstd=carve(rstd_off, S, F32), tmps=[carve(o, S, F32) for o in tmp_offs], sg=sg)

        def norm_chunk(nb, kc):
            sq = nb["sqs"][kc % 2]
            act(sq.v(), xT.v(kc * S, (kc + 1) * S), AF.Square)
            for tb in range(4):
                mm(bank(nb["sg"] * 4 + tb), onesd, sq.v(tb * 512, tb * 512 + 512), kc == 0, kc == KC - 1)

        def norm_finish(nb, gs_col, shift_col):
            rstd = nb["rstd"]
            act(rstd.v(), grp(nb["sg"]), AF.Ln, bias=epsc)
            act(rstd.v(), rstd.v(), AF.Exp, scale=-0.5)
            for kc in range(KC):
                tmp = nb["tmps"][kc % 2]
                tt(tmp.v(), xT.v(kc * S, (kc + 1) * S), rstd.v(), ALU.mult)
                act(hT.v(kc * S, (kc + 1) * S), tmp.v(), AF.Identity,
                    bias=sm.v(shift_col + kc, shift_col + kc + 1), scale=sm.v(gs_col + kc, gs_col + kc + 1))

        NB2 = lambda: norm_bufs(SCR, SCR + 8192, [SCR + 16384, SCR + 24576], sg=0)
        NB1 = lambda: norm_bufs(49152, 57344, [0, 8192])

        def proj(out_fn, slab, ncol0, kcn, rhs_fn, tbs=range(4), t0=0):
            for kc in range(kcn):
                w = slab(kc, ncol0, ncol0 + 128)
                for tb in tbs:
                    mm(out_fn(tb), w, rhs_fn(kc, tb), kc == 0, kc == kcn - 1)

        hT_rhs = lambda kc, tb: hT.v(kc * S + tb * 512, kc * S + tb * 512 + 512)

        def mod_slab_mm(l_, s_, cb=0):
            slab = wslab(w_ada_d[l_, :, s_ * 256:(s_ + 1) * 256], KC, 256)

            def half(h):
                j = s_ * 2 + h
                for kc in range(KC):
                    mm(bank(6, cb + j, cb + j + 1), slab(kc, h * 128, h * 128 + 128), smb.v(kc, kc + 1), kc == 0, kc == KC - 1)
            return [lambda: half(0), lambda: half(1)]

        def mod_finish(l_, mb_, c0=0, c1=48, cb=0):
            o_ = PK_L0 + l_ * LP
            tt(sm.v(mb_ + c0, mb_ + c1), bank(6, cb + c0, cb + c1), pk.v(o_ + O_BADA + c0, o_ + O_BADA + c1), ALU.add)
            if c0 == 0:
                stt(sm.v(mb_ + 48, mb_ + 56), sm.v(mb_ + 8, mb_ + 16), 1.0, pk.v(o_ + O_GMIX, o_ + O_GMIX + 8), ALU.add, ALU.mult)
                tsm(sm.v(mb_ + 64, mb_ + 65), pk.v(o_ + O_QG, o_ + O_QG + 1), 0.125)
            if c1 == 48:
                stt(sm.v(mb_ + 56, mb_ + 64), sm.v(mb_ + 32, mb_ + 40), 1.0, pk.v(o_ + O_GFFN, o_ + O_GFFN + 8), ALU.add, ALU.mult)

        for l in range(n_layers):
            pkl = PK_L0 + l * LP
            pcol = lambda o, n=1: pk.v(pkl + o, pkl + o + n)

            mb = (l % 2) * 80
            modT = lambda a, b=None, mb=mb: sm.v(mb + a, mb + (a + 1 if b is None else b))
            if l == 0:
                P.phase = 'L0.mod'
                for s_ in range(8):
                    for f_ in mod_slab_mm(0, s_):
                        f_()
                mod_finish(0, mb, 0, 16)
            SH_M, GATE_M, SH_F, GATE_F = 0, 16, 24, 40

            P.phase = f'L{l}.norm1'
            if l == 0:
                nb1 = NB1()
                for kc in range(KC):
                    norm_chunk(nb1, kc)
            norm_finish(nb1, mb + 48, mb + SH_M)

            attnT = carve(0, 4 * S, BF16)
            ucn = carve(16384, 4 * S, BF16)
            kTa = carve(16384, S, BF16)
            kTb = carve(20480, S, BF16)
            qTs = [carve(24576 + i * 4096, S, BF16) for i in range(2)]
            Vaug = carve(SCR, 16 * 2 * 192, BF16)
            Es = [carve(SCR + 12288 + i * 2048, K1, BF16) for i in range(3)]
            sqb = carve(SCR + 18432, K1, BF16)
            rstd_q = carve(SCR + 20480, K1, F32)
            qn = carve(SCR + 24576, K1, F32)
            t1 = carve(SCR + 28672, K1, F32)
            rden = carve(SCR + 32768, K1, F32)

            sqbs = [sqb, carve(SCR + 12288, K1, BF16)]
            rstds = [rstd_q, carve(SCR + 14336, K1, F32)]
            qns = [qn, rden]
            qbuf = [carve((c + 1) * 4096, S, BF16) for c in range(3)] + [qTs[0]]

            def qk_stages(job, slab, ncol0, gcol, dst, half):
                b_ = job % 2
                zp = pair(b_)
                sq_, rs_, qn_ = sqbs[b_], rstds[b_], qns[b_]
                c0 = half * K1

                def sA():
                    proj(lambda tb: pair(b_, tb * 512, tb * 512 + 512), slab, ncol0, KC,
                         lambda kc, tb: hT_rhs(kc, half * 2 + tb), tbs=range(2))

                def sB():
                    act(sq_.v(), pair(b_), AF.Square)

                def sC():
                    for i in range(2):
                        mm(pair(2, i * 512, i * 512 + 512), blk64, sq_.v(i * 512, i * 512 + 512), True, True)

                def sD():
                    act(rs_.v(), pair(2), AF.Ln, bias=epsc)
                    act(rs_.v(), rs_.v(), AF.Exp, scale=-0.5)

                def sE():
                    stt(qn_.v(), pair(b_), gcol, rs_.v(), ALU.mult, ALU.mult)

                def sF():
                    for i in range(2):
                        mm(pair(3, i * 512, i * 512 + 512), rotT, qn_.v(i * 512, i * 512 + 512), True, True)

                def sG():
                    tt(t1.v(), qn_.v(), cst.v(C_COS + c0, C_COS + c0 + K1), ALU.mult)
                    tt(qn_.v(), pair(3), cst.v(C_SIN + c0, C_SIN + c0 + K1), ALU.mult)
                    tt(dst.v(c0, c0 + K1), t1.v(), qn_.v(), ALU.add)
                return [sA, sB, sC, sD, sE, sF, sG]

            P.phase = f'L{l}.kv'
            rstate["pos"] = 0
            slab_kv = wslab(w_in_d[l, :, 512:768], KC, 256)
            slab_q = [wslab(w_in_d[l, :, i * 256:(i + 1) * 256], KC, 256) for i in range(2)]
            rstate["lo"] = rstate["pos"]
            assert rstate["lo"] == 6144
            memset(Vaug.v(), 1.0)
            for t in range(16):
                for kc in range(KC):
                    mm(ps.v(2048 + t * 128, 2048 + t * 128 + 128), hT.v(kc * S + t * 128, kc * S + t * 128 + 128),
                       slab_kv(kc, 128, 256), kc == 0, kc == KC - 1)
            vsrc = grp(1)
            vsrc.ap = vsrc.ap.rearrange("p (t g c) -> p t g c", t=16, g=2)
            vdst = Vaug.v()
            vdst.ap = vdst.ap.rearrange("p (t g c) -> p t g c", t=16, g=2)[:, :, :, 64:128]
            cp(vdst, vsrc)
            jobs = []
            for half in range(2):
                jobs.append(qk_stages(len(jobs), slab_kv, 0, pcol(O_KG), kTa, half))
            for c in range(4):
                for half in range(2):
                    jobs.append(qk_stages(len(jobs), slab_q[c // 2], (c % 2) * 128, sm.v(mb + 64, mb + 65), qbuf[c], half))
            SKEW = 4
            for t in range(len(jobs) * SKEW + 7):
                for j, job in enumerate(jobs):
                    st_ = t - j * SKEW
                    if 0 <= st_ < 7:
                        job[st_]()
                if t == 1 * SKEW + 7:
                    pass
            Z11 = carve(SCR + 24576, S, BF16)
            Z01 = carve(SCR + 28672, S, BF16)
            for tb in range(4):
                mm(bank(tb), swapb, kTa.v(tb * 512, tb * 512 + 512), True, True)
            memset(kTa.v(0, S, 64, 128), 0.0)
            act(kTb.v(0, S, 0, 64), ps.v(0, 2048, 0, 64), AF.Copy)
            act(Z01.v(0, S, 64, 128), ps.v(0, 2048, 64, 128), AF.Copy)
            memset(Z01.v(0, S, 0, 64), 0.0)
            memset(kTb.v(0, S, 64, 128), 0.0)
            kz = [[kTa, Z01], [kTb, Z11]]

            P.phase = f'L{l}.attn'
            accsb = rstd_q
            mods_pending = [(l + 1, s_, 0) for s_ in range(24)] if l + 1 < n_layers else []
            if l == 0:
                mods_pending = [(0, s_, 64) for s_ in range(8, 24)] + mods_pending
            mod_every = max(2, 256 // (len(mods_pending) + 1))
            mod_halves = []
            mods_done = False

            def mod_fin(l=l, mb=mb):
                if l == 0:
                    mod_finish(0, mb, 16, 48, 64)
                if l + 1 < n_layers:
                    mod_finish(l + 1, ((l + 1) % 2) * 80)
            gstep = 0

            def s_emit(qT, g, st, idx):
                par, qh, kb = st
                kX = kz[g][par]
                for i in range(2):
                    mm(pair(idx % 2, i * 512, i * 512 + 512),
                       kX.v(kb * 128, kb * 128 + 128),
                       qT.v(qh * K1 + i * 512, qh * K1 + i * 512 + 512), True, True)

            steps = [(c, par, qh, kb) for c in range(4) for par in range(2) for qh in range(2) for kb in range(16)]
            NS = len(steps)
            s_emit(qbuf[steps[0][0]], steps[0][0] // 2, steps[0][1:], 0)
            s_emit(qbuf[steps[1][0]], steps[1][0] // 2, steps[1][1:], 1)
            for si, (c, par, qh, kb) in enumerate(steps):
                g = c // 2
                E = Es[si % 3]
                act(E.v(), pair(si % 2), AF.Exp)
                if si + 2 < NS:
                    c2 = steps[si + 2][0]
                    s_emit(qbuf[c2], c2 // 2, steps[si + 2][1:], si + 2)
                vlo = 0 if par == 1 else 64
                vb = (kb * 2 + g) * 192 + vlo
                for i in range(2):
                    mm(pair(2, i * 512, i * 512 + 512), Vaug.v(vb, vb + 128),
                       E.v(i * 512, i * 512 + 512), kb == 0, kb == 15)
                if si < 8:
                    mm(bank(7), identb, hT.v(0, 512), True, True)
                if kb == 15:
                    orow0, drow0 = (0, 64) if par == 0 else (64, 0)
                    cp(accsb.v(0, 512), bank(4))
                    cp(accsb.v(512, K1), bank(5))
                    rd = rden.v(0, K1, orow0, orow0 + 64)
                    recip(rd, accsb.v(0, K1, drow0, drow0 + 64))
                    tt(attnT.v(c * S + qh * K1, c * S + qh * K1 + K1, orow0, orow0 + 64),
                       accsb.v(0, K1, orow0, orow0 + 64), rd, ALU.mult)
                gstep += 1
                if mod_halves:
                    mod_halves.pop(0)()
                elif mods_pending and gstep % mod_every == mod_every // 2:
                    mod_halves = mod_slab_mm(*mods_pending.pop(0))
                    mod_halves.pop(0)()
                if si == 40:
                    cp(Z11.v(0, S, 64, 128), kTb.v(0, S, 0, 64))
                    memset(Z11.v(0, S, 0, 64), 0.0)
                if not mods_done and not mods_pending and not mod_halves:
                    mods_done = True
                    mod_fin()
            while mods_pending or mod_halves:
                if not mod_halves:
                    mod_halves = mod_slab_mm(*mods_pending.pop(0))
                mod_halves.pop(0)()
            if not mods_done:
                mod_fin()
            rstate["lo"] = 0

            P.phase = f'L{l}.conv'
            sbuf_ = carve(SCR, S, BF16)
            UW = S + 32
            us = [carve(SCR + 4096 + i * (UW * 2), UW, BF16) for i in range(2)]
            diags = [carve(SCR + 4096 + 2 * UW * 2 + i * 7936, 31 * 128, BF16) for i in range(2)]
            TD = 8
            cacc = carve(SCR + 4096 + 2 * UW * 2 + 2 * 7936, S, F32)
            slab_a = [None, None]
            slab_b = [None, None]
            for c in range(4):
                if c % 2 == 0:
                    slab_a[0] = wslab(w_in_d[l, :, 768 + (c // 2) * 256: 768 + (c // 2) * 256 + 256], KC, 256)
                    slab_b[0] = wslab(w_in_d[l, :, 1280 + (c // 2) * 256: 1280 + (c // 2) * 256 + 256], KC, 256)
                u = us[c % 2]
                dg = diags[c % 2]
                for j in range(TD, 31):
                    tsm(dg.v(j * 128, j * 128 + 128), identb, pcol(O_DW + c * 31 + j))
                memset(u.v(0, 16), 0.0)
                memset(u.v(16 + S, UW), 0.0)
                proj(lambda tb: bank(4 + tb), slab_b[0], (c % 2) * 128, KC, hT_rhs)
                for th in range(2):
                    act(sbuf_.v(th * K1, th * K1 + K1), pair(2 + th), AF.Sigmoid)
                proj(lambda tb: bank(tb), slab_a[0], (c % 2) * 128, KC, hT_rhs)
                for th in range(2):
                    tt(u.v(16 + th * K1, 16 + th * K1 + K1), pair(th), sbuf_.v(th * K1, th * K1 + K1), ALU.mult)
                for j in range(TD):
                    src = u.v(j + 1, j + 1 + S)
                    wj = pcol(O_DW + c * 31 + j)
                    if j == 0:
                        tsm(cacc.v(), src, wj)
                    else:
                        stt(cacc.v(), src, wj, cacc.v(), ALU.mult, ALU.add)
                for tb in range(4):
                    for j in range(TD, 31):
                        mm(bank(4 + tb), dg.v(j * 128, j * 128 + 128), u.v(tb * 512 + j + 1, tb * 512 + j + 1 + 512), j == TD, j == 30)
                stt(ucn.v(c * S, (c + 1) * S), grp(1), pcol(O_DWB + c), cacc.v(), ALU.add, ALU.add)
            P.phase = f'L{l}.ln'
            sqs = [carve(SCR + i * 4096, S, BF16) for i in range(2)]
            m2 = carve(SCR + 8192, S, F32)
            tmps = [carve(SCR + 16384 + i * 8192, S, F32) for i in range(2)]
            for c in range(4):
                sq = sqs[c % 2]
                act(sq.v(), ucn.v(c * S, (c + 1) * S), AF.Square)
                for tb in range(4):
                    mm(bank(tb), onesc, ucn.v(c * S + tb * 512, c * S + tb * 512 + 512), c == 0, c == 3)
                for tb in range(4):
                    mm(bank(4 + tb), onesc, sq.v(tb * 512, tb * 512 + 512), c == 0, c == 3)
            act(m2.v(), grp(0), AF.Square)
            meanb = sqs[0]
            act(meanb.v(), grp(0), AF.Copy)
            tt(m2.v(), grp(1), m2.v(), ALU.subtract)
            act(m2.v(), m2.v(), AF.Ln, bias=epsc)
            act(m2.v(), m2.v(), AF.Exp, scale=-0.5)
            for c in range(4):
                gi_ = (c + 1) % 2
                for tb in range(4):
                    mm(bank(gi_ * 4 + tb), identb, ucn.v(c * S + tb * 512, c * S + tb * 512 + 512), True, False)
                    mm(bank(gi_ * 4 + tb), nidentb, meanb.v(tb * 512, tb * 512 + 512), False, True)
                tmp = tmps[c % 2]
                tt(tmp.v(), grp(gi_), m2.v(), ALU.mult)
                act(ucn.v(c * S, (c + 1) * S), tmp.v(), AF.Silu, bias=pcol(O_LNB + c), scale=pcol(O_LNG + c))

            P.phase = f'L{l}.merge'
            sgc = carve(SCR, S, BF16)
            sga = carve(SCR + 4096, S, BF16)
            mbuf = carve(SCR + 8192, S, BF16)
            tbuf = carve(SCR + 12288, S, F32)
            merged = carve(SCR + 20480, 4 * S, BF16)
            gcount = 0
            for hj in range(2):
                for jj in range(4):
                    j = hj * 4 + jj
                    if j % 2 == 0:
                        s_gc = wslab(w_in_d[l, :, 1792 + (j // 2) * 256: 1792 + (j // 2) * 256 + 256], KC, 256)
                        s_co = wslab(w_co_d[l, :, (j // 2) * 256:(j // 2) * 256 + 256], 4, 256)
                        s_ga = wslab(w_in_d[l, :, 2816 + (j // 2) * 256: 2816 + (j // 2) * 256 + 256], KC, 256)
                        s_ao = wslab(w_ao_d[l, :, (j // 2) * 256:(j // 2) * 256 + 256], 4, 256)
                    n0 = (j % 2) * 128
                    proj(lambda tb: bank(4 + tb), s_gc, n0, KC, hT_rhs)
                    act(sgc.v(), grp(1), AF.Sigmoid)
                    proj(lambda tb: bank(tb), s_co, n0, 4,
                         lambda kc, tb: ucn.v(kc * S + tb * 512, kc * S + tb * 512 + 512))
                    stt(mbuf.v(), grp(0), pcol(O_BCO + j), sgc.v(), ALU.add, ALU.mult)
                    proj(lambda tb: bank(4 + tb), s_ga, n0, KC, hT_rhs)
                    act(sga.v(), grp(1), AF.Sigmoid)
                    proj(lambda tb: bank(tb), s_ao, n0, 4,
                         lambda kc, tb: attnT.v(kc * S + tb * 512, kc * S + tb * 512 + 512))
                    tt(tbuf.v(), grp(0), sga.v(), ALU.mult)
                    tt(merged.v(jj * S, (jj + 1) * S), tbuf.v(), mbuf.v(), ALU.add)
                if hj == 1:
                    nb2 = NB2()
                for n in range(8):
                    if n % 2 == 0:
                        s_wo = wslab(w_out_d[l, hj * 512:(hj + 1) * 512, (n // 2) * 256:(n // 2) * 256 + 256], 4, 256)
                    def wo_mm(out_fn, n_, kcs, rhs_fn, tbs):
                        for kc in kcs:
                            w = s_wo(kc, (n_ % 2) * 128, (n_ % 2) * 128 + 128)
                            for tb in tbs:
                                mm(out_fn(tb), w, rhs_fn(kc, tb), kc == 0, kc == 3)
                    if hj == 0:
                        rhs_ = lambda kc, tb: merged.v(kc * S + tb * 512, kc * S + tb * 512 + 512)
                        gof = lambda n_: (lambda tb: bank(((n_ + 1) % 2) * 4 + tb))
                        if n == 0:
                            wo_mm(gof(0), 0, (0, 1, 2), rhs_, range(4))
                            wo_mm(gof(1), 1, (0, 1, 2), rhs_, range(4))
                            for n_ in (0, 1):
                                wo_mm(gof(n_), n_, (3,), rhs_, range(4))
                                stt(xT.v(n_ * S, (n_ + 1) * S), grp((n_ + 1) % 2), modT(GATE_M + n_),
                                    xT.v(n_ * S, (n_ + 1) * S), ALU.mult, ALU.add)
                        elif n >= 2:
                            wo_mm(gof(n), n, (0, 1, 2, 3), rhs_, range(4))
                            stt(xT.v(n * S, (n + 1) * S), grp((n + 1) % 2), modT(GATE_M + n),
                                xT.v(n * S, (n + 1) * S), ALU.mult, ALU.add)
                    else:
                        rh_ = lambda th: (lambda kc, tb: merged.v(kc * S + (th * 2 + tb) * 512, kc * S + (th * 2 + tb) * 512 + 512))
                        po_ = lambda th: (lambda tb: pair(2 + th, tb * 512, tb * 512 + 512))
                        if n == 0:
                            for th in range(2):
                                wo_mm(po_(th), n, (0, 1, 2), rh_(th), range(2))
                        for th in range(2):
                            wo_mm(po_(th), n, (3,) if n == 0 else (0, 1, 2, 3), rh_(th), range(2))
                            xs_ = xT.v(n * S + th * K1, n * S + th * K1 + K1)
                            stt(xs_, pair(2 + th), modT(GATE_M + n), xs_, ALU.mult, ALU.add)
                        if n >= 1:
                            norm_chunk(nb2, n - 1)
                if hj == 1:
                    norm_chunk(nb2, 7)

            P.phase = f'L{l}.norm2'
            norm_finish(nb2, mb + 56, mb + SH_F)
            P.phase = f'L{l}.ffn'
            actb = carve(0, 12 * S, BF16)
            sgs = [carve(12 * S * 2 + i * 4096, S, BF16) for i in range(2)]
            f0 = 0
            for hf, nf in enumerate((12, 10)):
                for ff in range(nf):
                    f = f0 + ff
                    if f % 2 == 0:
                        s_g = wslab(w_fi_d[l, :, (f // 2) * 256:(f // 2) * 256 + 256], KC, 256)
                        s_u = wslab(w_fi_d[l, :, DFF + (f // 2) * 256: DFF + (f // 2) * 256 + 256], KC, 256)
                    sg = sgs[f % 2]
                    proj(lambda tb: bank(tb), s_g, (f % 2) * 128, KC, hT_rhs)
                    act(sg.v(), grp(0), AF.Silu)
                    proj(lambda tb: bank(4 + tb), s_u, (f % 2) * 128, KC, hT_rhs)
                    tt(actb.v(ff * S, (ff + 1) * S), grp(1), sg.v(), ALU.mult)
                ovl = (hf == 1 and l + 1 < n_layers)
                if ovl:
                    nb1 = NB1()
                for n in range(8):
                    s_fo = wslab(w_fo_d[l, f0 * 128:(f0 + nf) * 128, n * 128:(n + 1) * 128], nf, 128)
                    if not ovl:
                        gi = n % 2
                        proj(lambda tb: bank(gi * 4 + tb), s_fo, 0, nf,
                             lambda kc, tb: actb.v(kc * S + tb * 512, kc * S + tb * 512 + 512))
                        stt(xT.v(n * S, (n + 1) * S), grp(gi), modT(GATE_F + n), xT.v(n * S, (n + 1) * S), ALU.mult, ALU.add)
                    else:
                        for th in range(2):
                            proj(lambda tb: pair(th, tb * 512, tb * 512 + 512), s_fo, 0, nf,
                                 lambda kc, tb: actb.v(kc * S + (th * 2 + tb) * 512, kc * S + (th * 2 + tb) * 512 + 512),
                                 tbs=range(2))
                            xs_ = xT.v(n * S + th * K1, n * S + th * K1 + K1)
                            stt(xs_, pair(th), modT(GATE_F + n), xs_, ALU.mult, ALU.add)
                        if n >= 1:
                            norm_chunk(nb1, n - 1)
                if ovl:
                    norm_chunk(nb1, 7)
                f0 += nf

        P.phase = 'final'
        fg = carve(0, D, F32)
        dma("sp", fg.v(), fg_d, "fg")
        junk = carve(4096, D, F32)
        ostg = [carve(8192 + i * 4096, D, F32) for i in range(2)]
        stat = carve(16384, 64, F32)
        memset(stat.v(), 0.0)
        outs = []
        for t in range(16):
            pr = t % 4
            for kc in range(KC):
                tr(pair(pr, kc * 128, kc * 128 + 128), xT.v(kc * S + t * 128, kc * S + t * 128 + 128), ident)
            ss = stat.v(2 * t, 2 * t + 1)
            sd = stat.v(2 * t + 1, 2 * t + 2)
            act(junk.v(), pair(pr), AF.Square, accum=ss)
            act(sd, ss, AF.Sqrt, bias=epsc, scale=1.0 / D)
            recip(sd, sd)
            o = ostg[t % 2]
            stt(o.v(), pair(pr), sd, fg.v(), ALU.mult, ALU.mult)
            outs.append(dma_out("sp", out_d[t * 128:(t + 1) * 128, :], o.v(), f"o{t % 2}"))
        if dbg and "sm" in dbg_d:
            outs.append(dma_out("sp", dbg_d["sm"], sm.v(), "dbgsm"))
            outs = outs[-3:]
        else:
            outs = outs[-2:]
        P.emit(ctx, final_waits=outs)
        build.stats = {k: len(v) for k, v in P.ops.items()}
        build.stats["sems"] = P.nsems
        build.prog = P
    return nc


def _consts():
    cst = np.zeros((128, NCST), np.float32)
    cst[:, C_ID:C_ID + 128] = np.eye(128, dtype=np.float32)
    R = np.zeros((128, 128), np.float32)
    for p in range(128):
        i = p % 32
        if i < 16:
            R[p, p + 16] = -1.0
        else:
            R[p, p - 16] = 1.0
    cst[:, C_ROT:C_ROT + 128] = R.T
    tpos = np.arange(S)
    row = (tpos // 64).astype(np.float32)
    col = (tpos % 64).astype(np.float32)
    inv = (np.float32(10000.0) ** (-np.arange(0, 32, 2, dtype=np.float32) / np.float32(32))).astype(np.float32)
    for p in range(128):
        i = p % 16
        seg = (p % 64) // 32
        pos = row if seg == 0 else col
        ang = (pos * inv[i]).astype(np.float32)
        cst[p, C_COS:C_COS + S] = np.cos(ang)
        cst[p, C_SIN:C_SIN + S] = np.sin(ang)
    cst[:, C_EPS] = EPS
    cst[:, C_ONE] = 1.0
    cb = np.zeros((128, NCSTB), np.float32)
    cb[:, B_ONESD:B_ONESD + 128] = 1.0 / D
    blk = np.zeros((128, 128), np.float32)
    blk[:64, :64] = 1.0 / 64
    blk[64:, 64:] = 1.0 / 64
    cb[:, B_BLK:B_BLK + 128] = blk
    cb[:, B_ONESC:B_ONESC + 128] = 1.0 / 512
    cb[:, B_ID:B_ID + 128] = np.eye(128, dtype=np.float32)
    sw = np.zeros((128, 128), np.float32)
    for p in range(128):
        sw[(p + 64) % 128, p] = 1.0
    cb[:, B_SWAP:B_SWAP + 128] = sw
    cb[:, B_NID:B_NID + 128] = -np.eye(128, dtype=np.float32)
    return cst, cb


def _pack(b, c, b_ada, norm_mix_g, q_norm_g, k_norm_g, conv_dw, conv_dw_b, conv_ln_g, conv_ln_b,
          b_conv_o, norm_ffn_g):
    pk = np.zeros((128, NPK), np.float32)
    col = lambda v: np.ascontiguousarray(v.reshape(-1, 128).T)
    pk[:, PK_C:PK_C + 8] = col(c[b])
    for l in range(DEPTH):
        o = PK_L0 + l * LP
        pk[:, o + O_GMIX:o + O_GMIX + 8] = col(norm_mix_g[l])
        pk[:, o + O_GFFN:o + O_GFFN + 8] = col(norm_ffn_g[l])
        pk[:, o + O_QG] = np.tile(q_norm_g[l], 2)
        pk[:, o + O_KG] = np.tile(k_norm_g[l], 2)
        dw = conv_dw[l].reshape(31, 4, 128).transpose(2, 1, 0).reshape(128, 124)
        pk[:, o + O_DW:o + O_DW + 124] = dw
        pk[:, o + O_DWB:o + O_DWB + 4] = col(conv_dw_b[l])
        pk[:, o + O_LNG:o + O_LNG + 4] = col(conv_ln_g[l])
        pk[:, o + O_LNB:o + O_LNB + 4] = col(conv_ln_b[l])
        pk[:, o + O_BCO:o + O_BCO + 8] = col(b_conv_o[l])
        pk[:, o + O_BADA:o + O_BADA + 48] = col(b_ada[l])
    return pk


_NC_CACHE = {}


def make_in_maps(cores, x, c, w_ada, b_ada, norm_mix_g, w_in, q_norm_g, k_norm_g, w_attn_o,
                 conv_dw, conv_dw_b, conv_ln_g, conv_ln_b, w_conv_o, b_conv_o, w_out,
                 norm_ffn_g, w_ffn_in, w_ffn_out, final_norm_g):
    f = lambda a: np.ascontiguousarray(np.asarray(a, dtype=np.float32))
    cst, cb = _consts()
    fg = np.ascontiguousarray(np.broadcast_to(f(final_norm_g)[None, :], (128, D)))
    shared = {"cst": cst, "cstb": cb, "fg": fg, "w_ada": f(w_ada), "w_in": f(w_in), "w_attn_o": f(w_attn_o),
              "w_conv_o": f(w_conv_o), "w_out": f(w_out), "w_ffn_in": f(w_ffn_in), "w_ffn_out": f(w_ffn_out)}
    x = f(x)
    maps = []
    for b in cores:
        m = dict(shared)
        m["x"] = np.ascontiguousarray(x[b])
        m["pk"] = _pack(b, f(c), f(b_ada), f(norm_mix_g), f(q_norm_g), f(k_norm_g), f(conv_dw), f(conv_dw_b),
                        f(conv_ln_g), f(conv_ln_b), f(b_conv_o), f(norm_ffn_g))
        maps.append(m)
    return maps


def kernel(**inputs):
    if "nc" not in _NC_CACHE:
        _NC_CACHE["nc"] = build(DEPTH)
    nc = _NC_CACHE["nc"]
    maps = make_in_maps(list(range(NCORES)), **inputs)
    res = run_bass_kernel_spmd(nc, maps, core_ids=list(range(NCORES)))
    out = np.stack([np.asarray(r["out"], dtype=np.float32) for r in res.results], axis=0)
    return out
```
